# Optimizing a Trainium2 kernel written in Bass

```python
import jax, jax.numpy as jnp
from jax import lax
import numpy as np

D_MODEL = 1024
BATCH = 4
SEQ = 8192
DEPTH = 2

SSM_INNER = D_MODEL
SSM_HEAD_DIM = 64
SSM_HEADS = SSM_INNER // SSM_HEAD_DIM
SSM_GROUPS = 2
SSM_STATE = 128
SSM_CONV = 5
SSM_CHUNK = 128
SSM_CONV_DIM = SSM_INNER + 2 * SSM_GROUPS * SSM_STATE
RWKV_DIM = D_MODEL // 2
RWKV_HEAD_DIM = 64
RWKV_HEADS = RWKV_DIM // RWKV_HEAD_DIM
RWKV_W_RANK = 64
RWKV_A_RANK = 64
RWKV_G_RANK = 128
RWKV_COLS = 3 * RWKV_DIM + RWKV_W_RANK + RWKV_A_RANK + RWKV_G_RANK
RWKV_DECAY_SCALE = 0.6065306597
RWKV_LN_EPS = 64e-5
HGRN_DIM = D_MODEL // 2
HGRN_HEAD_DIM = 128
HGRN_HEADS = HGRN_DIM // HGRN_HEAD_DIM
HGRN_CHUNK = 64
HGRN_COLS = 5 * HGRN_DIM
N_BRANCH = 3
GATE_COLS = N_BRANCH * D_MODEL
N_IN = SSM_INNER + SSM_CONV_DIM + 2 * SSM_HEADS + RWKV_COLS + HGRN_COLS + GATE_COLS
D_FF = (8 * D_MODEL + 3 * 256 - 1) // (3 * 256) * 256
NORM_EPS = 1e-6

kernel_name = "hybrid_ssd_rwkv7_hgrn2_gated_encoder"


def _split_points(sizes):
    pts, acc = [], 0
    for s in sizes[:-1]:
        acc += s
        pts.append(acc)
    return pts


def rmsnorm(x, w, eps=NORM_EPS):
    xf = x.astype(jnp.float32)
    y = xf * lax.rsqrt(jnp.mean(xf * xf, axis=-1, keepdims=True) + eps)
    return (y * w.astype(jnp.float32)).astype(x.dtype)


def head_rmsnorm(x, w, n_heads, eps=NORM_EPS):
    b, l, d = x.shape
    xf = x.astype(jnp.float32).reshape(b, l, n_heads, d // n_heads)
    y = xf * lax.rsqrt(jnp.mean(xf * xf, axis=-1, keepdims=True) + eps)
    return (y.reshape(b, l, d) * w.astype(jnp.float32)).astype(x.dtype)


def flip_t(t):
    return jnp.flip(t, axis=1)


def centred_depthwise_conv(x, w, b):
    pad = (w.shape[0] - 1) // 2
    y = lax.conv_general_dilated(x, w, window_strides=(1,), padding=[(pad, pad)],
                                 dimension_numbers=('NWC', 'WIO', 'NWC'),
                                 feature_group_count=x.shape[-1])
    return y + b


def ssd_chunked(x, dt, a, bm, cm):
    b, L, H, P = x.shape
    G, N = bm.shape[2], bm.shape[3]
    hg = H // G
    c = L // SSM_CHUNK
    Q = SSM_CHUNK
    x = x.reshape(b, c, Q, G, hg, P)
    dt = dt.reshape(b, c, Q, G, hg)
    bm = bm.reshape(b, c, Q, G, N)
    cm = cm.reshape(b, c, Q, G, N)
    a_cum = jnp.cumsum(dt * a.reshape(G, hg), axis=2)
    xdt = x * dt[..., None]
    seg = a_cum[:, :, :, None] - a_cum[:, :, None]
    mask = jnp.tril(jnp.ones((Q, Q), dtype=bool))[:, :, None, None]
    lmat = jnp.exp(jnp.where(mask, seg, -jnp.inf))
    cb = jnp.einsum('bcqgn,bcsgn->bcqsg', cm, bm)
    y_diag = jnp.einsum('bcqsgh,bcsghp->bcqghp', cb[..., None] * lmat, xdt)
    decay_to_end = jnp.exp(a_cum[:, :, -1:] - a_cum)
    chunk_states = jnp.einsum('bcsgn,bcsghp->bcghpn', bm, xdt * decay_to_end[..., None])
    chunk_decay = jnp.exp(a_cum[:, :, -1])

    def step(s, inp):
        st, dec = inp
        return dec[..., None, None] * s + st, s

    s0 = jnp.zeros((b, G, hg, P, N), dtype=x.dtype)
    _, prev = lax.scan(step, s0, (jnp.moveaxis(chunk_states, 1, 0), jnp.moveaxis(chunk_decay, 1, 0)))
    prev = jnp.moveaxis(prev, 0, 1)
    y_off = jnp.einsum('bcqgn,bcghpn->bcqghp', cm, prev) * jnp.exp(a_cum)[..., None]
    return (y_diag + y_off).reshape(b, L, H, P)


def ssd_branch(z, xbc, dt_raw, conv_w, conv_b, dt_bias, a_log, d_skip, norm_w):
    b, L, _ = z.shape
    xbc = jax.nn.silu(centred_depthwise_conv(xbc, conv_w, conv_b))
    xs, bm, cm = jnp.split(xbc, [SSM_INNER, SSM_INNER + SSM_GROUPS * SSM_STATE], axis=-1)
    xs = xs.reshape(b, L, SSM_HEADS, SSM_HEAD_DIM)
    bm = bm.reshape(b, L, SSM_GROUPS, SSM_STATE)
    cm = cm.reshape(b, L, SSM_GROUPS, SSM_STATE)
    dt = jax.nn.softplus(dt_raw.reshape(b, L, 2, SSM_HEADS) + dt_bias)
    a = -jnp.exp(a_log)
    y_f = ssd_chunked(xs, dt[:, :, 0], a[0], bm, cm)
    y_b = flip_t(ssd_chunked(flip_t(xs), flip_t(dt[:, :, 1]), a[1], flip_t(bm), flip_t(cm)))
    y = y_f + y_b + xs * d_skip[:, None]
    y = y.reshape(b, L, SSM_INNER) * jax.nn.silu(z)
    return rmsnorm(y, norm_w)


def token_shift_bidir(p, mu):
    prev = jnp.pad(p[:, :-1], ((0, 0), (1, 0), (0, 0)))
    nxt = jnp.pad(p[:, 1:], ((0, 0), (0, 1), (0, 0)))
    return p + mu[0] * (prev - p) + mu[1] * (nxt - p)


def rwkv7_scan(r, w, k, v, kk, a, reverse):
    b = r.shape[0]
    s0 = jnp.zeros((b, RWKV_HEADS, RWKV_HEAD_DIM, RWKV_HEAD_DIM), dtype=r.dtype)

    def step(s, inp):
        r_t, w_t, k_t, v_t, kk_t, a_t = inp
        sa = jnp.einsum('bhvk,bhk->bhv', s, -kk_t)
        s = (s * w_t[:, :, None, :] + sa[..., None] * (kk_t * a_t)[:, :, None, :]
             + v_t[..., None] * k_t[:, :, None, :])
        return s, jnp.einsum('bhvk,bhk->bhv', s, r_t)

    xs = (jnp.moveaxis(r, 1, 0), jnp.moveaxis(w, 1, 0), jnp.moveaxis(k, 1, 0),
          jnp.moveaxis(v, 1, 0), jnp.moveaxis(kk, 1, 0), jnp.moveaxis(a, 1, 0))
    _, y = lax.scan(step, s0, xs, reverse=reverse)
    return jnp.moveaxis(y, 0, 1)


def rwkv7_branch(p, mu, w0, w_up, a0, a_up, g_up, k_k, k_a, r_k, ln_w, ln_b):
    b, L, _ = p.shape
    p = token_shift_bidir(p, mu)
    r, k, v, wd, ad, gd = jnp.split(p, _split_points(
        [RWKV_DIM, RWKV_DIM, RWKV_DIM, RWKV_W_RANK, RWKV_A_RANK, RWKV_G_RANK]), axis=-1)

    def heads(t):
        return t.reshape(b, L, RWKV_HEADS, RWKV_HEAD_DIM)

    tw = jnp.tanh(wd)
    w_f = jnp.exp(-RWKV_DECAY_SCALE * jax.nn.sigmoid(w0[0] + tw @ w_up[0]))
    w_b = jnp.exp(-RWKV_DECAY_SCALE * jax.nn.sigmoid(w0[1] + tw @ w_up[1]))
    a = jax.nn.sigmoid(a0 + ad @ a_up)
    g = jax.nn.sigmoid(gd) @ g_up
    kk = heads(k * k_k).astype(jnp.float32)
    kk = (kk * lax.rsqrt(jnp.sum(kk * kk, axis=-1, keepdims=True) + 1e-12)).astype(p.dtype)
    k = k * (1.0 + (a - 1.0) * k_a)
    r, k, v, a = heads(r), heads(k), heads(v), heads(a)
    y = (rwkv7_scan(r, heads(w_f), k, v, kk, a, False)
         + rwkv7_scan(r, heads(w_b), k, v, kk, a, True))
    yf = y.astype(jnp.float32)
    mean = jnp.mean(yf, axis=-1, keepdims=True)
    var = jnp.mean(jnp.square(yf - mean), axis=-1, keepdims=True)
    yn = ((yf - mean) * lax.rsqrt(var + RWKV_LN_EPS)).reshape(b, L, RWKV_DIM)
    yn = (yn * ln_w.astype(jnp.float32) + ln_b.astype(jnp.float32)).astype(p.dtype)
    r_k_h = r_k.reshape(RWKV_HEADS, RWKV_HEAD_DIM)
    bonus = jnp.sum(r * k * r_k_h, axis=-1, keepdims=True) * v
    return (yn + bonus.reshape(b, L, RWKV_DIM)) * g


def hgrn2_chunk_scan(q, k, v, log_f):
    b, L, h, dk = q.shape
    dv = v.shape[-1]
    c = L // HGRN_CHUNK

    def chunks(t):
        return jnp.transpose(t.reshape(b, c, HGRN_CHUNK, h, t.shape[-1]), (1, 0, 3, 2, 4))

    mask = jnp.tril(jnp.ones((HGRN_CHUNK, HGRN_CHUNK), dtype=bool))[:, :, None]

    def step(s, inp):
        q_c, k_c, v_c, g_c = inp
        cum = jnp.cumsum(g_c, axis=2)
        o_inter = jnp.einsum('bhtk,bhkv->bhtv', q_c * jnp.exp(cum), s)
        diff = cum[:, :, :, None, :] - cum[:, :, None, :, :]
        decay = jnp.exp(jnp.where(mask, diff, -jnp.inf))
        scores = jnp.einsum('bhtsk,bhsk->bhts', q_c[:, :, :, None, :] * decay, k_c)
        o_intra = jnp.einsum('bhts,bhsv->bhtv', scores, v_c)
        last = cum[:, :, -1:, :]
        s = (jnp.exp(last[:, :, 0, :])[..., None] * s
             + jnp.einsum('bhsk,bhsv->bhkv', k_c * jnp.exp(last - cum), v_c))
        return s, o_inter + o_intra

    s0 = jnp.zeros((b, h, dk, dv), dtype=q.dtype)
    _, o = lax.scan(step, s0, (chunks(q), chunks(k), chunks(v), chunks(log_f)))
    return jnp.transpose(o, (1, 0, 3, 2, 4)).reshape(b, L, h, dv)


def hgrn2_branch(p, lb, norm_w):
    b, L, _ = p.shape
    q, f_fwd, f_bwd, i, g = jnp.split(p, 5, axis=-1)

    def heads(t):
        return t.reshape(b, L, HGRN_HEADS, HGRN_HEAD_DIM)

    def forget(f):
        ff = lb + (1.0 - lb) * jax.nn.sigmoid(f)
        return heads(jnp.log(ff)), heads(1.0 - ff)

    logf_f, k_f = forget(f_fwd)
    logf_b, k_b = forget(f_bwd)
    qh, ih = heads(q), heads(i)
    o_f = hgrn2_chunk_scan(qh, k_f, ih, logf_f)
    o_b = flip_t(hgrn2_chunk_scan(flip_t(qh), flip_t(k_b), flip_t(ih), flip_t(logf_b)))
    o = (o_f + o_b).reshape(b, L, HGRN_DIM)
    return head_rmsnorm(o, norm_w, HGRN_HEADS) * jax.nn.silu(g)


def setup_inputs(seed: int = 0) -> dict:
    key = jax.random.key(seed)
    ks = iter(jax.random.split(key, 48))
    f32 = jnp.float32

    def nrm(shape, fan_in, scale=1.0):
        return scale * fan_in ** -0.5 * jax.random.normal(next(ks), shape, f32)

    def unif(shape, lo, hi):
        return jax.random.uniform(next(ks), shape, f32, lo, hi)

    def gain(shape):
        return 1.0 + 0.05 * jax.random.normal(next(ks), shape, f32)

    x = jax.random.normal(next(ks), (BATCH, SEQ, D_MODEL), f32)
    norm1_w = gain((DEPTH, D_MODEL))
    w_in = nrm((DEPTH, D_MODEL, N_IN), D_MODEL)
    ssm_conv_w = nrm((DEPTH, SSM_CONV, 1, SSM_CONV_DIM), SSM_CONV)
    ssm_conv_b = 0.02 * jax.random.normal(next(ks), (DEPTH, SSM_CONV_DIM), f32)
    dt0 = jnp.exp(unif((DEPTH, 2, SSM_HEADS), float(np.log(1e-3)), float(np.log(1e-1))))
    ssm_dt_bias = dt0 + jnp.log(-jnp.expm1(-dt0))
    ssm_a_log = jnp.log(unif((DEPTH, 2, SSM_HEADS), 1.0, 16.0))
    ssm_d = gain((DEPTH, SSM_HEADS))
    ssm_norm_w = gain((DEPTH, SSM_INNER))
    rwkv_mu = unif((DEPTH, 2, RWKV_COLS), 0.0, 0.5)
    rwkv_w0 = 0.5 * jax.random.normal(next(ks), (DEPTH, 2, RWKV_DIM), f32)
    rwkv_w_up = nrm((DEPTH, 2, RWKV_W_RANK, RWKV_DIM), RWKV_W_RANK, 0.5)
    rwkv_a0 = 0.1 * jax.random.normal(next(ks), (DEPTH, RWKV_DIM), f32)
    rwkv_a_up = nrm((DEPTH, RWKV_A_RANK, RWKV_DIM), RWKV_A_RANK, 0.1)
    rwkv_g_up = nrm((DEPTH, RWKV_G_RANK, RWKV_DIM), RWKV_G_RANK)
    rwkv_k_k = 0.85 + 0.05 * jax.random.normal(next(ks), (DEPTH, RWKV_DIM), f32)
    rwkv_k_a = gain((DEPTH, RWKV_DIM))
    rwkv_r_k = 0.1 * jax.random.normal(next(ks), (DEPTH, RWKV_DIM), f32)
    rwkv_ln_w = gain((DEPTH, RWKV_DIM))
    rwkv_ln_b = 0.02 * jax.random.normal(next(ks), (DEPTH, RWKV_DIM), f32)
    hgrn_lb_logits = 0.1 * jax.random.normal(next(ks), (DEPTH, HGRN_DIM), f32)
    hgrn_norm_w = gain((DEPTH, HGRN_DIM))
    w_branch_ssm = nrm((DEPTH, SSM_INNER, D_MODEL), SSM_INNER)
    w_branch_rwkv = nrm((DEPTH, RWKV_DIM, D_MODEL), RWKV_DIM)
    w_branch_hgrn = nrm((DEPTH, HGRN_DIM, D_MODEL), HGRN_DIM)
    w_out = nrm((DEPTH, D_MODEL, D_MODEL), D_MODEL)
    norm2_w = gain((DEPTH, D_MODEL))
    ffn_w_in = nrm((DEPTH, D_MODEL, 2 * D_FF), D_MODEL)
    ffn_w_down = nrm((DEPTH, D_FF, D_MODEL), D_FF)
    final_norm_w = gain((D_MODEL,))
    return {"x": x, "norm1_w": norm1_w, "w_in": w_in,
            "ssm_conv_w": ssm_conv_w, "ssm_conv_b": ssm_conv_b, "ssm_dt_bias": ssm_dt_bias,
            "ssm_a_log": ssm_a_log, "ssm_d": ssm_d, "ssm_norm_w": ssm_norm_w,
            "rwkv_mu": rwkv_mu, "rwkv_w0": rwkv_w0, "rwkv_w_up": rwkv_w_up, "rwkv_a0": rwkv_a0,
            "rwkv_a_up": rwkv_a_up, "rwkv_g_up": rwkv_g_up, "rwkv_k_k": rwkv_k_k,
            "rwkv_k_a": rwkv_k_a, "rwkv_r_k": rwkv_r_k, "rwkv_ln_w": rwkv_ln_w,
            "rwkv_ln_b": rwkv_ln_b, "hgrn_lb_logits": hgrn_lb_logits, "hgrn_norm_w": hgrn_norm_w,
            "w_branch_ssm": w_branch_ssm, "w_branch_rwkv": w_branch_rwkv,
            "w_branch_hgrn": w_branch_hgrn, "w_out": w_out, "norm2_w": norm2_w,
            "ffn_w_in": ffn_w_in, "ffn_w_down": ffn_w_down, "final_norm_w": final_norm_w}


def reference(x, norm1_w, w_in, ssm_conv_w, ssm_conv_b, ssm_dt_bias, ssm_a_log, ssm_d,
              ssm_norm_w, rwkv_mu, rwkv_w0, rwkv_w_up, rwkv_a0, rwkv_a_up, rwkv_g_up,
              rwkv_k_k, rwkv_k_a, rwkv_r_k, rwkv_ln_w, rwkv_ln_b, hgrn_lb_logits, hgrn_norm_w,
              w_branch_ssm, w_branch_rwkv, w_branch_hgrn, w_out, norm2_w, ffn_w_in,
              ffn_w_down, final_norm_w):
    b, L, _ = x.shape
    lb_p = jax.nn.softmax(hgrn_lb_logits.astype(jnp.float32), axis=0)
    lower_bounds = (jnp.cumsum(lb_p, axis=0) - lb_p[0]).astype(x.dtype)
    in_splits = _split_points([SSM_INNER, SSM_CONV_DIM, 2 * SSM_HEADS, RWKV_COLS, HGRN_COLS,
                               GATE_COLS])
    for l in range(DEPTH):
        xn = rmsnorm(x, norm1_w[l])
        p = xn @ w_in[l]
        z, xbc, dt_raw, p_rwkv, p_hgrn, p_gate = jnp.split(p, in_splits, axis=-1)
        y_ssm = ssd_branch(z, xbc, dt_raw, ssm_conv_w[l], ssm_conv_b[l], ssm_dt_bias[l],
                           ssm_a_log[l], ssm_d[l], ssm_norm_w[l]) @ w_branch_ssm[l]
        y_rwkv = rwkv7_branch(p_rwkv, rwkv_mu[l], rwkv_w0[l], rwkv_w_up[l], rwkv_a0[l],
                              rwkv_a_up[l], rwkv_g_up[l], rwkv_k_k[l], rwkv_k_a[l],
                              rwkv_r_k[l], rwkv_ln_w[l], rwkv_ln_b[l]) @ w_branch_rwkv[l]
        y_hgrn = hgrn2_branch(p_hgrn, lower_bounds[l], hgrn_norm_w[l]) @ w_branch_hgrn[l]
        gates = jax.nn.sigmoid(p_gate).reshape(b, L, N_BRANCH, D_MODEL)
        mixed = gates[:, :, 0] * y_ssm + gates[:, :, 1] * y_rwkv + gates[:, :, 2] * y_hgrn
        x = x + mixed @ w_out[l]
        h_gate, h_up = jnp.split(rmsnorm(x, norm2_w[l]) @ ffn_w_in[l], 2, axis=-1)
        x = x + (jax.nn.silu(h_gate) * h_up) @ ffn_w_down[l]
    return rmsnorm(x, final_norm_w)
```

```python
import contextlib
import numpy as np
import concourse.bass as bass
import concourse.mybir as mybir
from concourse.bass_utils import run_bass_kernel_spmd

F32 = mybir.dt.float32
BF16 = mybir.dt.bfloat16
AF = mybir.ActivationFunctionType
ALU = mybir.AluOpType
AX = mybir.AxisListType

ENGS = ("pe", "act", "dve", "pool", "sp")


class Prog:
    def __init__(self, nc):
        self.nc = nc
        self.q = {e: [] for e in ENGS}
        self.last_w = {}
        self.readers = {}
        self.dma_cnt = {}
        self.seen = {e: {} for e in ENGS}
        self.nwaits = 0

    def _deps(self, eng, reads, writes):
        deps = set()
        idx = len(self.q[eng])
        for k in reads:
            t = self.last_w.get(k)
            if t is not None:
                if t[0] == eng:
                    if eng == "pe" or t[1] < idx - 1:
                        continue
                deps.add(t)
        for k in writes:
            t = self.last_w.get(k)
            if t is not None and t[0] != eng:
                deps.add(t)
            for t in self.readers.get(k, ()):
                if t[0] != eng:
                    deps.add(t)
        out = []
        seen = self.seen[eng]
        best = {}
        for (sk, v) in deps:
            if seen.get(sk, 0) >= v:
                continue
            if best.get(sk, 0) < v:
                best[sk] = v
        for sk, v in best.items():
            seen[sk] = v
            out.append((sk, v))
        return out

    def op(self, eng, fn, reads=(), writes=()):
        deps = self._deps(eng, reads, writes)
        idx = len(self.q[eng])
        self.q[eng].append(dict(fn=fn, deps=deps, dma=None))
        tok = (eng, idx + 1)
        for k in reads:
            self.readers.setdefault(k, []).append(tok)
        for k in writes:
            self.last_w[k] = tok
            self.readers[k] = []
        return tok

    def dma(self, eng, fns, sem, reads=(), writes=()):
        deps = self._deps(eng, reads, writes)
        n0 = self.dma_cnt.get(sem, 0)
        n1 = n0 + len(fns)
        self.dma_cnt[sem] = n1
        for i, fn in enumerate(fns):
            self.q[eng].append(dict(fn=fn, deps=deps if i == 0 else [], dma=sem))
        tok = (("dma", sem), 16 * n1)
        for k in reads:
            self.readers.setdefault(k, []).append(tok)
        for k in writes:
            self.last_w[k] = tok
            self.readers[k] = []
        return tok

    def cc(self, eng, fn, sem, reads=(), writes=()):
        deps = self._deps(eng, reads, writes)
        assert sem not in self.dma_cnt
        self.dma_cnt[sem] = 0
        self.cc_sems = getattr(self, "cc_sems", set()); self.cc_sems.add(sem)
        self.q[eng].append(dict(fn=fn, deps=deps, dma=sem, inc=None))
        tok = (("dma", sem), 1)
        for k in reads:
            self.readers.setdefault(k, []).append(tok)
        for k in writes:
            self.last_w[k] = tok
            self.readers[k] = []
        return tok

    def barrier(self):
        toks = []
        for e in ENGS:
            last = 0
            for i, rec in enumerate(self.q[e]):
                if rec["fn"] is not None and rec["dma"] is None:
                    last = i + 1
            if last:
                toks.append((e, last))
        for s_, n in self.dma_cnt.items():
            if s_ in getattr(self, "cc_sems", ()):
                toks.append((("dma", s_), 1))
            elif n:
                toks.append((("dma", s_), 16 * n))
        for e in ENGS:
            seen = self.seen[e]
            deps = []
            for (sk, v) in toks:
                if sk == e or seen.get(sk, 0) >= v:
                    continue
                seen[sk] = v
                deps.append((sk, v))
            self.q[e].append(dict(fn=None, deps=deps, dma=None))
        self.last_w = {}
        self.readers = {}

    def wait_all(self, eng, keys):
        deps = self._deps(eng, (), keys)
        self.q[eng].append(dict(fn=None, deps=deps, dma=None))

    def emit(self):
        nc = self.nc
        signaled = {e: set() for e in ENGS}
        for e in ENGS:
            for rec in self.q[e]:
                for (sk, v) in rec["deps"]:
                    if sk in ENGS:
                        signaled[sk].add(v)
        valmap = {}
        for e in ENGS:
            s = sorted(signaled[e])
            valmap[e] = {v: i + 1 for i, v in enumerate(s)}
        dma_sems = sorted(self.dma_cnt.keys())
        with contextlib.ExitStack() as es:
            esem = {e: es.enter_context(nc.semaphore(f"sem_{e}")) for e in ENGS}
            dsem = {s: es.enter_context(nc.semaphore(f"dsem_{s}")) for s in dma_sems}
            block = es.enter_context(nc.Block())
            engmap = {"pe": "tensor", "act": "scalar", "dve": "vector", "pool": "gpsimd", "sp": "sync"}

            def make(e):
                def body(engobj):
                    for i, rec in enumerate(self.q[e]):
                        for (sk, v) in rec["deps"]:
                            if sk in ENGS:
                                engobj.wait_ge(esem[sk], valmap[sk][v])
                            else:
                                engobj.wait_ge(dsem[sk[1]], v)
                            self.nwaits += 1
                        if rec["fn"] is None:
                            continue
                        ins = rec["fn"](engobj)
                        if rec["dma"] is not None:
                            if rec.get("inc", 16) is None:
                                ins.then_inc(dsem[rec["dma"]])
                            else:
                                ins.then_inc(dsem[rec["dma"]], 16)
                        elif (i + 1) in valmap[e]:
                            ins.then_inc(esem[e], 1)
                return body

            for e in ENGS:
                if self.q[e]:
                    getattr(block, engmap[e])(make(e))


class T_:
    __slots__ = ("ap", "key", "psum")

    def __init__(self, ap, key, psum=False):
        self.ap = ap; self.key = key; self.psum = psum

    def __getitem__(self, idx):
        return T_(self.ap[idx], self.key, self.psum)

    def v(self, pat, **kw):
        return T_(self.ap.rearrange(pat, **kw), self.key, self.psum)

    def bc(self, shape):
        return T_(self.ap.to_broadcast(list(shape)), self.key, self.psum)

    def us(self, axis):
        return T_(self.ap.unsqueeze(axis), self.key, self.psum)

    def cast(self, dt):
        return T_(self.ap.bitcast(dt), self.key, self.psum)

    @property
    def shape(self):
        return self.ap.shape


def _rw(outs, ins):
    r, w = [], []
    for t in ins:
        if isinstance(t, T_):
            (w if t.psum else r).append(t.key)
    for t in outs:
        if isinstance(t, T_):
            w.append(t.key)
    return r, w


def _a(x):
    return x.ap if isinstance(x, T_) else x


class Ops:
    def __init__(self, P):
        self.P = P

    def act(self, out, in_, func, bias=0.0, scale=1.0, accum=None):
        r, w = _rw([out, accum], [in_, bias, scale])
        kw = {}
        if accum is not None:
            kw["accum_out"] = _a(accum)
        self.P.op("act", lambda e: e.activation(out=_a(out), in_=_a(in_), func=func, bias=_a(bias), scale=_a(scale), **kw), r, w)

    def tt(self, eng, out, a, b, op):
        r, w = _rw([out], [a, b])
        self.P.op(eng, lambda e: e.tensor_tensor(out=_a(out), in0=_a(a), in1=_a(b), op=op), r, w)

    def ts(self, eng, out, a, s1, op0, s2=None, op1=None):
        r, w = _rw([out], [a, s1, s2])
        if op1 is None:
            self.P.op(eng, lambda e: e.tensor_scalar(out=_a(out), in0=_a(a), scalar1=_a(s1), scalar2=None, op0=op0), r, w)
        else:
            self.P.op(eng, lambda e: e.tensor_scalar(out=_a(out), in0=_a(a), scalar1=_a(s1), scalar2=_a(s2), op0=op0, op1=op1), r, w)

    def stt(self, out, a, s, b, op0, op1):
        r, w = _rw([out], [a, s, b])
        self.P.op("dve", lambda e: e.scalar_tensor_tensor(out=_a(out), in0=_a(a), scalar=_a(s), in1=_a(b), op0=op0, op1=op1), r, w)

    def cp(self, eng, out, in_):
        r, w = _rw([out], [in_])
        if eng == "act":
            self.P.op(eng, lambda e: e.copy(out=_a(out), in_=_a(in_)), r, w)
        else:
            self.P.op(eng, lambda e: e.tensor_copy(out=_a(out), in_=_a(in_)), r, w)

    def memset(self, eng, out, val):
        r, w = _rw([out], [])
        self.P.op(eng, lambda e: e.memset(_a(out), val), r, w)

    def mm(self, out, lhsT, rhs, start=True, stop=True):
        r, w = _rw([out], [lhsT, rhs])
        self.P.op("pe", lambda e: e.matmul(_a(out), lhsT=_a(lhsT), rhs=_a(rhs), start=start, stop=stop, skip_group_check=True), r, w)

    def tr(self, out, in_, ident):
        r, w = _rw([out], [in_, ident])
        self.P.op("pe", lambda e: e.transpose(out=_a(out), in_=_a(in_), identity=_a(ident)), r, w)

    def scan(self, out, d0, d1, init, op0, op1):
        r, w = _rw([out], [d0, d1, init])
        self.P.op("dve", lambda e: e.tensor_tensor_scan(out=_a(out), data0=_a(d0), data1=_a(d1), initial=_a(init), op0=op0, op1=op1), r, w)

    def dma(self, eng, sem, pairs, extra_r=(), extra_w=()):
        r, w = [], []
        fns = []
        for (o, i) in pairs:
            if o.key is not None:
                w.append(o.key)
            if i.key is not None:
                r.append(i.key)
            fns.append((lambda oo, ii: (lambda e: e.dma_start(out=oo, in_=ii)))(_a(o), _a(i)))
        self.P.dma(eng, fns, sem, list(r) + list(extra_r), list(w) + list(extra_w))


class Arena:
    def __init__(self, big_ap, nbytes):
        self.big = big_ap; self.n = nbytes; self.off = 0; self.uid = 0

    def alloc(self, name, free_shape, dt, parts=128):
        esz = 4 if dt == F32 else 2
        n = 1
        for s in free_shape:
            n *= s
        nb = (n * esz + 31) // 32 * 32
        assert self.off + nb <= self.n, f"SBUF arena overflow at {name}: {self.off}+{nb} > {self.n}"
        a = self.big[:, self.off // 4:(self.off + nb) // 4]
        self.off += nb
        if dt != F32:
            a = a.bitcast(dt)
        a = a[:, 0:n]
        if len(free_shape) == 2:
            a = a.rearrange("p (a b) -> p a b", a=free_shape[0])
        elif len(free_shape) == 3:
            a = a.rearrange("p (a b c) -> p a b c", a=free_shape[0], b=free_shape[1])
        if parts != 128:
            a = a[0:parts]
        self.uid += 1
        return T_(a, f"{name}#{self.uid}")

    def mark(self):
        return self.off

    def release(self, m):
        self.off = m

D = 1024
NIN = 10016
DFF = 2816
G = 512
GT = G // 128
EPS = 1e-6
DSC = 0.6065306597
NCOLP = 132
NROWP = 4688
ARENA = 190 * 1024

WIN_BLOCKS = [("lr", 2592 + 1536, 256), ("rr", 2592, 512), ("rk", 2592 + 512, 512), ("rv", 2592 + 1024, 512),
              ("sx0", 1024, 512), ("sx1", 1536, 512), ("sbc", 2048, 512),
              ("hq", 4384, 512), ("hff", 4896, 512), ("hfb", 5408, 512),
              ("z0", 0, 512), ("z1", 512, 512), ("dt", 2560, 32), ("hi", 5920, 512), ("hg", 6432, 512)] + \
             [(f"g{j}", 6944 + 512 * j, 512) for j in range(6)]
WIN_IDX = {n: i for i, (n, _, _) in enumerate(WIN_BLOCKS)}
FIN_BLOCKS = [(f"fg{b}", 512 * b, min(512, 2816 - 512 * b)) for b in range(6)] + \
             [(f"fu{b}", 2816 + 512 * b, min(512, 2816 - 512 * b)) for b in range(6)]


def make_consts():
    p = np.arange(128)[:, None]
    q = np.arange(128)[None, :]
    c = {}
    c["ident"] = (p == q)
    c["tri_f"] = (p <= q)
    c["tri_b"] = (p >= q)
    c["str_f"] = (p > q)
    c["str_b"] = (p < q)
    c["ones"] = np.ones((128, 128))
    c["negm_f"] = np.where(q >= p, 0.0, -30000.0)
    c["negm_b"] = np.where(q <= p, 0.0, -30000.0)
    ls = p % 64
    lt = q % 64
    same = (p // 64) == (q // 64)
    for d, before_strict, before_incl in (("f", ls < lt, ls <= lt), ("b", ls > lt, ls >= lt)):
        mu = np.where(q < 64, -1.0 * before_strict, 1.0 * before_incl)
        mk = np.where(q < 64, 1.0 * before_strict, 1.0 * before_incl)
        c["mu_" + d] = mu
        c["mk_" + d] = mk
        ss = np.arange(64)[None, :]
        lt2 = (np.arange(128) % 64)[:, None]
        c["mx_" + d] = np.concatenate([-1.0 * ((ss < lt2) if d == "f" else (ss > lt2)), np.zeros((128, 64))], 1)
        c["mh_" + d] = 1.0 * (same & before_incl)
    c["bones"] = 1.0 * same
    bs = np.zeros((128, 128)); bs[:64, 0] = 1; bs[64:, 1] = 1
    c["bsel"] = bs
    names = list(c.keys())
    arr = np.concatenate([np.asarray(c[n], np.float32) for n in names], 1)
    sel = np.zeros((128, 16, 128), np.float32)
    for h in range(16):
        sel[h, h, :] = 1.0
    arr = np.concatenate([arr, sel.reshape(128, -1)], 1)
    off = {n: i * 128 for i, n in enumerate(names)}
    off["sel"] = len(names) * 128
    return np.ascontiguousarray(arr), off


CONSTS, COFF = make_consts()
NCONST = CONSTS.shape[1]


def build(T, depth, dbg=(), stop=None, skip=(), interleave=True, npairs=0):
    NT = T // 128
    NG = T // G
    nc = bass.Bass("TRN2", target_bir_lowering=False)
    dbg = set(dbg)

    def din(name, shape, dt=F32):
        return nc.dram_tensor(name, list(shape), dt, kind="ExternalInput").ap()

    def dscr(name, shape, dt):
        kind = "ExternalOutput" if name in dbg else "Internal"
        return nc.dram_tensor(name, list(shape), dt, kind=kind).ap()

    x_in = din("x", [T, D])
    w_in = din("w_in", [depth, D, NIN])
    w_bs = din("w_branch_ssm", [depth, 1024, D])
    w_br = din("w_branch_rwkv", [depth, 512, D])
    w_bh = din("w_branch_hgrn", [depth, 512, D])
    w_o = din("w_out", [depth, D, D])
    w_fi = din("ffn_w_in", [depth, D, 2 * DFF])
    w_fd = din("ffn_w_down", [depth, DFF, D])
    w_up = din("rwkv_w_up", [depth, 2, 64, 512])
    a_up = din("rwkv_a_up", [depth, 64, 512])
    g_up = din("rwkv_g_up", [depth, 128, 512])
    colp = din("colp", [depth, 128, NCOLP])
    rowp = din("rowp", [depth, 1, NROWP])
    fnw = din("fnw", [1, D])
    consts_d = din("consts", [128, NCONST])
    out_d = nc.dram_tensor("out", [T, D], F32, kind="ExternalOutput").ap()
    paired = npairs > 0
    if paired:
        xhalo = din("xhalo", [2, D])
        groups = [[2 * i, 2 * i + 1] for i in range(npairs)]
        EXs = nc.dram_tensor("EXs", [128, 1792], F32).ap()
        EXr = nc.dram_tensor("EXr", [128, 1792], F32).ap()
        HX = nc.dram_tensor("HX", [128, 16], F32).ap()
        HXr = nc.dram_tensor("HXr", [128, 16], F32).ap()
        HD = nc.dram_tensor("HD", [2, D], F32).ap()

    WIN = dscr("WIN", [depth, len(WIN_BLOCKS), 128, 8, 512], BF16)
    WBS = dscr("WBS", [depth, 2, 128, 8, 512], BF16)
    WBR = dscr("WBR", [depth, 2, 128, 4, 512], BF16)
    WBH = dscr("WBH", [depth, 2, 128, 4, 512], BF16)
    WO = dscr("WO", [depth, 2, 128, 8, 512], BF16)
    FIN = dscr("FIN", [depth, 12, 128, 8, 512], BF16)
    FDN = dscr("FDN", [depth, 2, 128, 22, 512], BF16)
    XS = dscr("XS", [T, 1024], BF16)
    BTM = dscr("BTM", [T, 256], BF16)
    BTs = dscr("BTs", [2, 128, T], BF16)
    CTs = dscr("CTs", [2, 128, T], BF16)
    DTs = dscr("DTs", [T, 32], F32)
    ZS = dscr("ZS", [T, 1024], BF16)
    RW = dscr("RW", [16, 128, T], BF16)
    SW = dscr("SW", [8, 128, T], F32)
    VR = dscr("VR", [T, 512], BF16)
    GR = dscr("GR", [T, 512], BF16)
    BC = dscr("BC", [T, 8], F32)
    QH = dscr("QH", [4, 128, T], BF16)
    KH = dscr("KH", [8, 128, T], BF16)
    LF = dscr("LF", [8, 128, T], F32)
    IH = dscr("IH", [T, 512], BF16)
    GH = dscr("GH", [T, 512], BF16)
    GTs = dscr("GTs", [T, 3072], BF16)
    YS = dscr("YS", [2, T, 1024], F32)
    YR = dscr("YR", [2, T, 512], F32)
    YH = dscr("YH", [2, T, 512], F32)
    XM = dscr("XM", [T, D], F32)
    X1 = dscr("X1", [T, D], F32)

    def DR(ap, key=None):
        return T_(ap, key)

    es = contextlib.ExitStack()
    big = es.enter_context(nc.sbuf_tensor("big", [128, ARENA // 4], F32))
    A = Arena(big, ARENA)
    PS = [T_(es.enter_context(nc.psum_tensor(f"ps{i}", [128, 512], F32))[:], f"ps{i}", True) for i in range(8)]
    P = Prog(nc)
    O = Ops(P)
    cnt = [0]

    def rr(engs):
        cnt[0] += 1
        return engs[cnt[0] % len(engs)]

    cst = A.alloc("cst", [NCONST], F32)
    O.dma("sp", "ld_c", [(cst, DR(consts_d[:, :]))])

    def cf(name, w=128):
        o = COFF[name]
        return cst[:, o:o + w]

    identb = A.alloc("identb", [128], BF16)
    negmb = {d: A.alloc("negm" + d, [128], BF16) for d in "fb"}
    bonesb = A.alloc("bonesb", [128], BF16)
    bselb = A.alloc("bselb", [2], BF16)
    onesw = A.alloc("onesw", [512], F32)
    O.cp("dve", identb, cf("ident"))
    for d in "fb":
        O.cp("dve", negmb[d], cf("negm_" + d))
    O.cp("dve", bonesb, cf("bones"))
    O.cp("dve", bselb, cf("bsel", 2))
    O.memset("pool", onesw, 1.0)
    SEL = cst[0:16, COFF["sel"]:COFF["sel"] + 2048].v("p (h s) -> p h s", h=16)
    fnwb = A.alloc("fnwb", [D], F32)
    O.dma("sp", "ld_f", [(fnwb, DR(fnw[0:1, :].to_broadcast([128, D])))])

    cp_ = A.alloc("colp", [NCOLP], F32)
    rp = A.alloc("rowp", [NROWP], F32)
    der = A.alloc("der", [64], F32)
    arow = A.alloc("arow", [32], F32)
    lrw = A.alloc("lrw", [1024], BF16)
    gupb = A.alloc("gupb", [512], BF16)
    c0 = der[:, 0:14]; omka = der[:, 14:18]; lb = der[:, 18:22]; oml = der[:, 22:26]
    cw = lambda c, k: cp_[:, c * 5 + k:c * 5 + k + 1]
    cb = lambda c: cp_[:, 60 + c:61 + c]
    mu = lambda c, j: cp_[:, 72 + c * 2 + j:73 + c * 2 + j]
    w0c = lambda d, c: cp_[:, 100 + d * 4 + c:101 + d * 4 + c]
    a0c = lambda c: cp_[:, 108 + c:109 + c]
    kkc = lambda c: cp_[:, 112 + c:113 + c]
    kac = lambda c: cp_[:, 116 + c:117 + c]
    rkc = lambda c: cp_[:, 120 + c:121 + c]
    n1w = rp[:, 0:1024]; n2w = rp[:, 1024:2048]; snw = rp[:, 2048:3072]
    lnw = rp[:, 3072:3584]; lnb = rp[:, 3584:4096]; hnw = rp[:, 4096:4608]
    dtb = rp[:, 4608:4640]; alog = rp[:, 4640:4672]; dskip = rp[:, 4672:4688]

    pers_mark = A.mark()

    def prologue():
        m = A.mark()
        stf = [A.alloc(f"stf{i}", [8, 512], F32) for i in range(2)]
        stb = [A.alloc(f"stb{i}", [8, 512], BF16) for i in range(2)]
        u = [0]

        def unit(src, K0, kc, c0_, ncols, dst):
            i = u[0] % 2
            u[0] += 1
            s_ap = src[K0:K0 + kc * 128, c0_:c0_ + ncols].rearrange("(k p) n -> p k n", p=128)
            O.dma("sp", f"pl{i}", [(stf[i][:, 0:kc, 0:ncols], DR(s_ap))])
            O.cp(rr(["dve", "act"]), stb[i][:, 0:kc, 0:ncols], stf[i][:, 0:kc, 0:ncols])
            O.dma("pool", f"ps{i}", [(DR(dst[:, 0:kc, 0:ncols]), stb[i][:, 0:kc, 0:ncols])])

        for l in range(depth):
            for bi, (nm, c0_, ncols) in enumerate(WIN_BLOCKS):
                unit(w_in[l], 0, 8, c0_, ncols, WIN[l, bi])
            for cbk in range(2):
                unit(w_bs[l], 0, 8, cbk * 512, 512, WBS[l, cbk])
                unit(w_br[l], 0, 4, cbk * 512, 512, WBR[l, cbk])
                unit(w_bh[l], 0, 4, cbk * 512, 512, WBH[l, cbk])
                unit(w_o[l], 0, 8, cbk * 512, 512, WO[l, cbk])
                for k0 in (0, 8, 16):
                    kc = min(8, 22 - k0)
                    unit(w_fd[l], k0 * 128, kc, cbk * 512, 512, FDN[l, cbk][:, k0:k0 + kc, :])
            for bi, (nm, c0_, ncols) in enumerate(FIN_BLOCKS):
                unit(w_fi[l], 0, 8, c0_, ncols, FIN[l, bi])
        P.barrier()
        A.release(m)

    def layer_setup(l):
        m = A.mark()
        O.dma("sp", "ld_p", [(cp_, DR(colp[l])), (rp, DR(rowp[l, 0:1, :].to_broadcast([128, NROWP])))])
        stg = A.alloc("lrstg", [1024], F32)
        O.dma("sp", "ld_c2", [(stg[0:64, 0:512], DR(w_up[l, 0])), (stg[0:64, 512:1024], DR(w_up[l, 1])),
                              (stg[64:128, 0:512], DR(a_up[l]))])
        O.cp("dve", lrw[0:64, :], stg[0:64, :])
        O.cp("dve", lrw[64:128, 0:512], stg[64:128, 0:512])
        stg2 = A.alloc("gstg", [512], F32)
        O.dma("sp", "ld_c3", [(stg2, DR(g_up[l]))])
        O.cp("dve", gupb, stg2)
        muv = cp_[:, 72:100].v("p (c j) -> p c j", j=2)
        O.tt("dve", c0, muv[:, :, 0], muv[:, :, 1], ALU.add)
        O.ts("dve", c0, c0, -1.0, ALU.mult, 1.0, ALU.add)
        O.ts("dve", omka, cp_[:, 116:120], -1.0, ALU.mult, 1.0, ALU.add)
        if l == 0:
            O.memset("dve", lb, 0.0)
        else:
            O.tt("dve", lb, cp_[:, 128:132], cp_[:, 124:128], ALU.subtract)
            O.act(lb, lb, AF.Sigmoid)
        O.ts("dve", oml, lb, -1.0, ALU.mult, 1.0, ALU.add)
        O.act(arow, alog, AF.Exp)
        O.ts("dve", arow, arow, -1.0, ALU.mult)
        P.barrier()
        A.release(m)

    def norm_T(xin, npart, wrow, xn, sq, ss, outT_cols, ps_bank):
        O.act(sq[0:npart], xin[0:npart], AF.Square, accum=ss[0:npart])
        O.act(ss[0:npart], ss[0:npart], AF.Ln, bias=EPS, scale=1.0 / D)
        O.act(ss[0:npart], ss[0:npart], AF.Exp, scale=-0.5)
        O.stt(xn[0:npart], xin[0:npart], ss[0:npart, 0:1], wrow[0:npart], ALU.mult, ALU.mult)
        psb = ps_bank.cast(BF16)
        for kc in range(8):
            O.tr(psb[:, kc * 128:kc * 128 + npart], xn[0:npart, kc * 128:(kc + 1) * 128], identb[0:npart, 0:npart])
        return psb

    def phaseA(l, xsrc, rhalo=None):
        m = A.mark()
        xin = [A.alloc(f"xin{i}", [D], F32) for i in range(2)]
        xh = A.alloc("xh", [D], F32)
        sq = A.alloc("sq", [D], BF16)
        ss = A.alloc("ss", [1], F32)
        xn = [A.alloc(f"xn{i}", [D], BF16) for i in range(2)]
        xnT = [A.alloc(f"xnT{i}", [8, G], BF16) for i in range(2)]
        xnTh = [A.alloc(f"xnTh{i}", [8, 4], BF16) for i in range(2)]
        wst = [A.alloc(f"wst{i}", [8, 512], BF16) for i in range(3)]
        rawc = [A.alloc(f"rawc{i}", [G + 4], BF16) for i in range(3)]
        dgc = A.alloc("dgc", [12, 5, 128], BF16)
        dgs = A.alloc("dgs", [14, 3, 128], BF16)
        for c in range(12):
            for k in range(5):
                O.ts("dve", dgc[:, c, k, :], cf("ident"), cw(c, k), ALU.mult)
        for c in range(14):
            O.ts("dve", dgs[:, c, 0, :], cf("ident"), c0[:, c:c + 1], ALU.mult)
            O.ts("dve", dgs[:, c, 1, :], cf("ident"), mu(c, 0), ALU.mult)
            O.ts("dve", dgs[:, c, 2, :], cf("ident"), mu(c, 1), ALU.mult)
        fmb = [A.alloc(f"fmb{i}", [G], BF16) for i in range(4)]
        fmf = [A.alloc(f"fmf{i}", [G], F32) for i in range(3)]
        xsT = A.alloc("xsT", [8, G], BF16)
        bT = A.alloc("bT", [2, G], BF16)
        tmb = [A.alloc(f"tmb{i}", [1024], BF16) for i in range(3)]
        tmf = [A.alloc(f"tmf{i}", [32], F32) for i in range(2)]
        aT = A.alloc("aT", [4, G], F32)
        rT = A.alloc("rT", [4, G], BF16)
        vT = A.alloc("vT", [4, G], BF16)
        twad = A.alloc("twad", [G], BF16)
        sg = A.alloc("sg", [G], BF16)
        kf = A.alloc("kf", [G], F32)
        t1 = A.alloc("t1", [G], F32)
        t2 = A.alloc("t2", [G], F32)
        sqb = A.alloc("sqb", [G], BF16)
        kmb = A.alloc("kmb", [G], BF16)
        prodT = A.alloc("prodT", [G], BF16)
        bcst = A.alloc("bcst", [GT, 8], F32)
        ctr = dict(w=0, raw=0, acc=0, fmb=0, fmf=0, tmb=0, tmf=0, fm=0, misc=0)

        def nxt(name, n):
            ctr[name] += 1
            return ctr[name] % n

        def misc_ps():
            return PS[4 + nxt("misc", 3)]

        for gi in range(NG):
            t0 = gi * G
            xt_ = xnT[gi % 2]
            xth = xnTh[gi % 2]
            for j in range(GT):
                xi = xin[j % 2]
                O.dma("sp", f"ldx{j % 2}", [(xi, DR(xsrc[t0 + j * 128:t0 + (j + 1) * 128, :]))])
                psb = norm_T(xi, 128, n1w, xn[j % 2], sq, ss, None, PS[3])
                O.cp(rr(["act", "dve"]), xt_[:, :, j * 128:(j + 1) * 128], psb.v("p (k t) -> p k t", k=8))
            O.memset("pool", xh[0:4], 0.0)
            prs = []
            if t0 > 0:
                prs.append((xh[0:2], DR(xsrc[t0 - 2:t0, :])))
            if t0 + G < T:
                prs.append((xh[2:4], DR(xsrc[t0 + G:t0 + G + 2, :])))
            elif rhalo is not None:
                prs.append((xh[2:4], DR(rhalo[0:2, :], "rhalo")))
            if prs:
                O.dma("sp", "ldxh", prs)
            psb = norm_T(xh, 4, n1w, xn[0], sq, ss, None, PS[3])
            O.cp("dve", xth, psb.v("p (k t) -> p k t", k=8)[:, :, 0:4])

            def load_w(bi):
                w = wst[nxt("w", 3)]
                O.dma("sp", f"ldw{ctr['w'] % 3}", [(w, DR(WIN[l, bi]))])
                return w

            def fm_chunk(w, c, halo):
                ps = PS[(0, 1, 3)[nxt("fm", 3)]]
                for kc in range(8):
                    O.mm(ps[:, 0:G], w[:, kc, c * 128:(c + 1) * 128], xt_[:, kc, :], start=(kc == 0), stop=(kc == 7))
                if not halo:
                    return ps, None
                hoff = (ctr["fm"] % 8) * 4
                ph = PS[2][:, hoff:hoff + 4]
                for kc in range(8):
                    O.mm(ph, w[:, kc, c * 128:(c + 1) * 128], xth[:, kc, :], start=(kc == 0), stop=(kc == 7))
                rc = rawc[nxt("raw", 3)]
                O.cp("act", rc[:, 2:G + 2], ps[:, 0:G])
                O.cp("dve", rc[:, 0:2], ph[:, 0:2])
                O.cp("dve", rc[:, G + 2:G + 4], ph[:, 2:4])
                return ps, rc

            def shift_mm(rc, c):
                pc = misc_ps()
                O.mm(pc[:, 0:G], dgs[:, c, 0, :], rc[:, 2:G + 2], start=True, stop=False)
                O.mm(pc[:, 0:G], dgs[:, c, 1, :], rc[:, 1:G + 1], start=False, stop=False)
                O.mm(pc[:, 0:G], dgs[:, c, 2, :], rc[:, 3:G + 3], start=False, stop=True)
                return pc

            def g_shifted(rc, c, out, after=None):
                pc = shift_mm(rc, c)
                yield
                O.cp(rr(["act", "dve"]), out, pc[:, 0:G])
                if after is not None:
                    after()

            def g_conv(rc, c, out, after=None):
                pc = misc_ps()
                for k in range(5):
                    O.mm(pc[:, 0:G], dgc[:, c, k, :], rc[:, k:k + G], start=(k == 0), stop=(k == 4))
                yield
                O.act(out, pc[:, 0:G], AF.Silu, bias=cb(c))
                if after is not None:
                    after()

            def pipeline(tasks, depth=2):
                live = []
                it = iter(tasks)
                done = False
                while True:
                    if not done and len(live) < depth:
                        try:
                            live.append(next(it)())
                        except StopIteration:
                            done = True
                    if not live:
                        if done:
                            break
                        continue
                    for g_ in list(live):
                        try:
                            next(g_)
                        except StopIteration:
                            live.remove(g_)

            def store_fm(dst3, ci, src, sem):
                O.dma("pool", sem, [(DR(dst3[ci][:, t0:t0 + G]), src)])

            w = load_w(WIN_IDX["lr"])
            ps, rc = fm_chunk(w, 0, True)
            pc = shift_mm(rc, 12)
            O.act(twad[0:64], pc[0:64, 0:G], AF.Tanh)
            O.cp("dve", twad[64:128], pc[64:128, 0:G])
            for d in range(2):
                for j in range(4):
                    pm = misc_ps()
                    O.mm(pm[:, 0:G], lrw[0:64, d * 512 + j * 128:d * 512 + (j + 1) * 128], twad[0:64, :])
                    f = fmf[nxt("fmf", 3)]
                    O.act(f, pm[:, 0:G], AF.Sigmoid, bias=w0c(d, j))
                    store_fm(SW, d * 4 + j, f, f"sfmf{ctr['fmf'] % 3}")
            for j in range(4):
                pm = misc_ps()
                O.mm(pm[:, 0:G], lrw[64:128, j * 128:(j + 1) * 128], twad[64:128, :])
                O.act(aT[:, j, :], pm[:, 0:G], AF.Sigmoid, bias=a0c(j))
            ps, rc = fm_chunk(w, 1, True)
            pc = shift_mm(rc, 13)
            O.act(sg, pc[:, 0:G], AF.Sigmoid)
            for j in range(GT):
                pm = misc_ps()
                O.mm(pm[:, 0:512], sg[:, j * 128:(j + 1) * 128], gupb)
                tb = tmb[nxt("tmb", 3)]
                O.cp(rr(["act", "dve"]), tb[:, 0:512], pm[:, 0:512])
                O.dma("pool", f"stmb{ctr['tmb'] % 3}", [(DR(GR[t0 + j * 128:t0 + (j + 1) * 128, :]), tb[:, 0:512])])
            w = load_w(WIN_IDX["rr"])
            def task_r(c, w=w):
                def gen():
                    ps, rc = fm_chunk(w, c, True)
                    yield
                    yield from g_shifted(rc, c, rT[:, c, :], lambda: store_fm(RW, 0 * 4 + c, rT[:, c, :], "srT"))
                return gen
            pipeline([task_r(c) for c in range(4)])
            w = load_w(WIN_IDX["rk"])
            for c in range(4):
                ps, rc = fm_chunk(w, c, True)
                pc = shift_mm(rc, 4 + c)
                O.cp("act", kf, pc[:, 0:G])
                O.act(sqb, kf, AF.Square, scale=kkc(c))
                pm = misc_ps()
                O.mm(pm[:, 0:G], bonesb, sqb)
                O.act(t1, pm[:, 0:G], AF.Ln, bias=1e-12)
                O.act(t1, t1, AF.Exp, scale=-0.5)
                kkb = fmb[nxt("fmb", 4)]
                O.stt(kkb, kf, kkc(c), t1, ALU.mult, ALU.mult)
                store_fm(RW, 2 * 4 + c, kkb, f"sfmb{ctr['fmb'] % 4}")
                ub = fmb[nxt("fmb", 4)]
                O.tt("dve", ub, kkb, aT[:, c, :], ALU.mult)
                store_fm(RW, 3 * 4 + c, ub, f"sfmb{ctr['fmb'] % 4}")
                O.ts("dve", t2, aT[:, c, :], kac(c), ALU.mult, omka[:, c:c + 1], ALU.add)
                kmb_ = fmb[nxt("fmb", 4)]
                O.tt("dve", kmb_, kf, t2, ALU.mult)
                store_fm(RW, 1 * 4 + c, kmb_, f"sfmb{ctr['fmb'] % 4}")
                O.stt(prodT, rT[:, c, :], rkc(c), kmb_, ALU.mult, ALU.mult)
                for j in range(GT):
                    O.mm(PS[7][:, j * 8 + 2 * c:j * 8 + 2 * c + 2], prodT[:, j * 128:(j + 1) * 128], bselb)
            O.cp("dve", bcst, PS[7][:, 0:GT * 8].v("p (j e) -> p j e", j=GT))
            O.dma("pool", "sbc", [(DR(BC[t0:t0 + G, :].rearrange("(j p) e -> p j e", p=128)), bcst)])
            w = load_w(WIN_IDX["rv"])
            def task_v(c, w=w):
                def gen():
                    ps, rc = fm_chunk(w, c, True)
                    yield
                    yield from g_shifted(rc, 8 + c, vT[:, c, :])
                return gen
            pipeline([task_v(c) for c in range(4)])
            for j in range(GT):
                pm = misc_ps().cast(BF16)
                for c in range(4):
                    O.tr(pm[:, c * 128:(c + 1) * 128], vT[:, c, j * 128:(j + 1) * 128], identb)
                tb = tmb[nxt("tmb", 3)]
                O.cp(rr(["act", "dve"]), tb[:, 0:512], pm[:, 0:512])
                O.dma("pool", f"stmb{ctr['tmb'] % 3}", [(DR(VR[t0 + j * 128:t0 + (j + 1) * 128, :]), tb[:, 0:512])])
            xtasks = []
            for bi, nm in enumerate(("sx0", "sx1")):
                def task_x(bi, nm, c):
                    def gen():
                        if c == 0:
                            wx[bi] = load_w(WIN_IDX[nm])
                        ps, rc = fm_chunk(wx[bi], c, True)
                        yield
                        yield from g_conv(rc, bi * 4 + c, xsT[:, bi * 4 + c, :])
                    return gen
                xtasks += [task_x(bi, nm, c) for c in range(4)]
            wx = {}
            pipeline(xtasks)
            for j in range(GT):
                pm = misc_ps().cast(BF16)
                for c in range(8):
                    O.tr(pm[:, c * 128:(c + 1) * 128], xsT[:, c, j * 128:(j + 1) * 128], identb)
                tb = tmb[nxt("tmb", 3)]
                O.cp(rr(["act", "dve"]), tb, pm)
                O.dma("pool", f"stmb{ctr['tmb'] % 3}", [(DR(XS[t0 + j * 128:t0 + (j + 1) * 128, :]), tb)])
            w = load_w(WIN_IDX["sbc"])
            def task_bc(c, w=w):
                def gen():
                    ps, rc = fm_chunk(w, c, True)
                    yield
                    if c < 2:
                        yield from g_conv(rc, 8 + c, bT[:, c, :], lambda: store_fm(BTs, c, bT[:, c, :], "sbT"))
                    else:
                        f = fmb[nxt("fmb", 4)]
                        sem = f"sfmb{ctr['fmb'] % 4}"
                        yield from g_conv(rc, 8 + c, f, lambda: store_fm(CTs, c - 2, f, sem))
                return gen
            pipeline([task_bc(c) for c in range(4)])
            for j in range(GT):
                pm = misc_ps().cast(BF16)
                for c in range(2):
                    O.tr(pm[:, c * 128:(c + 1) * 128], bT[:, c, j * 128:(j + 1) * 128], identb)
                tb = tmb[nxt("tmb", 3)]
                O.cp(rr(["act", "dve"]), tb[:, 0:256], pm[:, 0:256])
                O.dma("pool", f"stmb{ctr['tmb'] % 3}", [(DR(BTM[t0 + j * 128:t0 + (j + 1) * 128, :]), tb[:, 0:256])])
            w = load_w(WIN_IDX["hq"])
            for c in range(4):
                ps, _ = fm_chunk(w, c, False)
                f = fmb[nxt("fmb", 4)]
                O.cp(rr(["act", "dve"]), f, ps[:, 0:G])
                store_fm(QH, c, f, f"sfmb{ctr['fmb'] % 4}")
            for d, nm in enumerate(("hff", "hfb")):
                w = load_w(WIN_IDX[nm])
                for c in range(4):
                    ps, _ = fm_chunk(w, c, False)
                    O.act(t1, ps[:, 0:G], AF.Sigmoid)
                    O.ts("dve", t2, t1, oml[:, c:c + 1], ALU.mult, lb[:, c:c + 1], ALU.add)
                    f = fmf[nxt("fmf", 3)]
                    O.act(f, t2, AF.Ln)
                    store_fm(LF, d * 4 + c, f, f"sfmf{ctr['fmf'] % 3}")
                    kb = fmb[nxt("fmb", 4)]
                    O.ts("dve", kb, t2, -1.0, ALU.mult, 1.0, ALU.add)
                    store_fm(KH, d * 4 + c, kb, f"sfmb{ctr['fmb'] % 4}")
            def tm_block(nm, ncols, post):
                w = load_w(WIN_IDX[nm])
                for j in range(GT):
                    pm = misc_ps()
                    for kc in range(8):
                        O.mm(pm[:, 0:ncols], xt_[:, kc, j * 128:(j + 1) * 128], w[:, kc, 0:ncols], start=(kc == 0), stop=(kc == 7))
                    post(j, pm)

            def post_store(func, dst, col0):
                def f(j, pm):
                    tb = tmb[nxt("tmb", 3)]
                    if func is None:
                        O.cp(rr(["act", "dve"]), tb[:, 0:512], pm[:, 0:512])
                    else:
                        O.act(tb[:, 0:512], pm[:, 0:512], func)
                    O.dma("pool", f"stmb{ctr['tmb'] % 3}", [(DR(dst[t0 + j * 128:t0 + (j + 1) * 128, col0:col0 + 512]), tb[:, 0:512])])
                return f

            tm_block("z0", 512, post_store(AF.Silu, ZS, 0))
            tm_block("z1", 512, post_store(AF.Silu, ZS, 512))

            def post_dt(j, pm):
                tf = tmf[nxt("tmf", 2)]
                O.tt("dve", tf, pm[:, 0:32], dtb, ALU.add)
                O.act(tf, tf, AF.Exp)
                O.act(tf, tf, AF.Ln, bias=1.0)
                O.dma("pool", f"stmf{ctr['tmf'] % 2}", [(DR(DTs[t0 + j * 128:t0 + (j + 1) * 128, :]), tf)])
            tm_block("dt", 32, post_dt)
            tm_block("hi", 512, post_store(None, IH, 0))
            tm_block("hg", 512, post_store(AF.Silu, GH, 0))
            for b in range(6):
                tm_block(f"g{b}", 512, post_store(AF.Sigmoid, GTs, 512 * b))
        P.barrier()
        A.release(m)

    id2 = A.alloc("id2", [64], F32)
    O.tt("dve", id2, cf("ident")[:, 0:64], cf("ident")[:, 64:128], ALU.add)
    pers_mark2 = A.mark()

    def scan_phase(l, d):
        m = A.mark()
        di = 0 if d == "f" else 1
        mloc = 31 if d == "f" else 32
        order = (0, 1) if d == "f" else (1, 0)
        tiles = list(range(NT)) if d == "f" else list(range(NT - 1, -1, -1))
        L = []
        for s in range(2):
            L.append(dict(
                xs=A.alloc(f"lxs{s}", [1024], BF16), btm=A.alloc(f"lbtm{s}", [256], BF16),
                bT=A.alloc(f"lbT{s}", [2, 128], BF16), cT=A.alloc(f"lcT{s}", [2, 128], BF16),
                dt=A.alloc(f"ldt{s}", [32], F32),
                rw=A.alloc(f"lrw{s}", [4, 4, 128], BF16), sw=A.alloc(f"lsw{s}", [4, 128], F32),
                vr=A.alloc(f"lvr{s}", [512], BF16),
                qh=A.alloc(f"lqh{s}", [4, 128], BF16), kh=A.alloc(f"lkh{s}", [4, 128], BF16),
                lf=A.alloc(f"llf{s}", [4, 128], F32), ih=A.alloc(f"lih{s}", [512], BF16)))

        def load(ti, s):
            b = L[s]
            t0 = ti * 128
            tsl = slice(t0, t0 + 128)
            prs = [(b["xs"], DR(XS[tsl, :])), (b["btm"], DR(BTM[tsl, :])),
                   (b["bT"], DR(BTs[:, :, tsl].rearrange("g n t -> n g t"))),
                   (b["cT"], DR(CTs[:, :, tsl].rearrange("g n t -> n g t"))),
                   (b["dt"], DR(DTs[tsl, :]))]
            O.dma("sp", f"lsA{s}", prs)
            prs = [(b["rw"][:, q, :, :], DR(RW[q * 4:(q + 1) * 4, :, tsl].rearrange("c p t -> p c t"))) for q in range(4)]
            prs += [(b["sw"], DR(SW[di * 4:(di + 1) * 4, :, tsl].rearrange("c p t -> p c t"))), (b["vr"], DR(VR[tsl, :]))]
            O.dma("sp", f"lsB{s}", prs)
            prs = [(b["qh"], DR(QH[:, :, tsl].rearrange("c p t -> p c t"))),
                   (b["kh"], DR(KH[di * 4:(di + 1) * 4, :, tsl].rearrange("c p t -> p c t"))),
                   (b["lf"], DR(LF[di * 4:(di + 1) * 4, :, tsl].rearrange("c p t -> p c t"))),
                   (b["ih"], DR(IH[tsl, :]))]
            O.dma("sp", f"lsC{s}", prs)

        dtA = A.alloc("dtA", [16], F32); nacum = A.alloc("nacum", [16], F32)
        Eall = A.alloc("Eall", [48], F32); w1 = A.alloc("w1", [16], F32)
        acT = A.alloc("acT", [128], F32)
        cbT = A.alloc("cbT", [2, 128], BF16)
        xdt = A.alloc("xdt", [1024], BF16); xdtd = A.alloc("xdtd", [1024], BF16)
        LT = [A.alloc(f"LT{i}", [4, 128], BF16) for i in range(2)]
        GTb = [A.alloc(f"GTb{i}", [4, 128], BF16) for i in range(2)]
        nacT = A.alloc("nacT", [128], F32)
        tmpf = A.alloc("tmpf", [512], F32)
        yss = [A.alloc(f"yss{i}", [1024], F32) for i in range(2)]
        Sf = A.alloc("Sf", [2, 512], F32); Sbf = A.alloc("Sbf", [2, 512], BF16)
        if not (paired and d == "b"):
            O.memset("dve", Sf, 0.0); O.memset("pool", Sbf, 0.0)

        def ssd_tile(ti, s):
            b = L[s]
            xs_, btm_, bT_, cT_, dt_ = b["xs"], b["btm"], b["bT"], b["cT"], b["dt"]
            dtd = dt_[:, di * 16:(di + 1) * 16]
            O.tt("dve", dtA, dtd, arow[:, di * 16:(di + 1) * 16], ALU.mult)
            psA = PS[0]
            O.mm(psA[:, 0:16], cf("tri_" + d), dtA)
            O.mm(psA[:, 16:32], cf("str_" + d), dtA)
            O.mm(psA[:, 32:48], cf("ones"), dtA)
            O.mm(psA[0:16, 64:192], dtA, cf("tri_" + d))
            O.ts("dve", nacum, psA[:, 0:16], -1.0, ALU.mult)
            O.act(Eall, psA[:, 0:48], AF.Exp)
            O.cp("act", acT[0:16], psA[0:16, 64:192])
            O.ts("dve", nacT[0:16], psA[0:16, 64:192], -1.0, ALU.mult)
            yield
            psB = PS[0]
            for g in range(2):
                O.mm(psB[:, g * 128:(g + 1) * 128], bT_[:, g, :], cT_[:, g, :])
            O.cp("act", cbT, psB[:, 0:256].v("p (g q) -> p g q", g=2))
            O.tt("dve", w1, dtd, Eall[:, 16:32], ALU.mult)
            xs3 = xs_.v("p (h e) -> p h e", h=16)
            O.tt("dve", xdt.v("p (h e) -> p h e", h=16), xs3, dtd.us(2).bc([128, 16, 64]), ALU.mult)
            O.tt("pool", xdtd.v("p (h e) -> p h e", h=16), xs3, w1.us(2).bc([128, 16, 64]), ALU.mult)
            yield
            ys_ = yss[ti % 2]
            for hq in range(4):
                g = hq // 2
                pseg = PS[1 + hq % 2]
                for i in range(4):
                    h = hq * 4 + i
                    reg = pseg[:, i * 128:(i + 1) * 128]
                    O.mm(reg, SEL[:, h, :], acT[0:16], start=True, stop=False)
                    O.mm(reg, nacT[0:16], SEL[:, h, :], start=False, stop=False)
                    O.mm(reg, identb, negmb[d], start=False, stop=True)
                lt = LT[hq % 2]
                O.act(lt, pseg.v("p (i q) -> p i q", i=4), AF.Exp)
                gt = GTb[hq % 2]
                O.tt("dve", gt, lt, cbT[:, g:g + 1, :].bc([128, 4, 128]), ALU.mult)
                yield
                for i in range(4):
                    h = hq * 4 + i
                    O.mm(PS[3][:, (h % 8) * 64:(h % 8 + 1) * 64], gt[:, i, :], xdt[:, h * 64:(h + 1) * 64], start=True, stop=True)
                if hq % 2 == 1:
                    O.cp("act", ys_[:, g * 512:(g + 1) * 512], PS[3][:, 0:512])
                yield
            for g in range(2):
                O.mm(PS[0][:, 0:512], cT_[:, g, :], Sbf[:, g, :])
                yield
                O.tt("dve", tmpf.v("p (h e) -> p h e", h=8), PS[0][:, 0:512].v("p (h e) -> p h e", h=8),
                     Eall[:, g * 8:(g + 1) * 8].us(2).bc([128, 8, 64]), ALU.mult)
                O.tt("dve", ys_[:, g * 512:(g + 1) * 512], ys_[:, g * 512:(g + 1) * 512], tmpf, ALU.add)
            O.dma("pool", f"sys{ti % 2}", [(DR(YS[di, ti * 128:(ti + 1) * 128, :]), ys_)])
            for g in range(2):
                O.mm(PS[0][:, 0:512], btm_[:, g * 128:(g + 1) * 128], xdtd[:, g * 512:(g + 1) * 512])
                yield
                S3 = Sf[:, g, :].v("p (h e) -> p h e", h=8)
                O.tt("dve", S3, S3, Eall[:, 32 + g * 8:32 + (g + 1) * 8].us(2).bc([128, 8, 64]), ALU.mult)
                O.tt("dve", Sf[:, g, :], Sf[:, g, :], PS[0][:, 0:512], ALU.add)
                O.cp("act", Sbf[:, g, :], Sf[:, g, :])
            yield

        rcs = A.alloc("rcs", [4, 128], F32); rex = A.alloc("rex", [4, 128], F32)
        rRS = A.alloc("rRS", [4, 128], F32)
        rEa = rcs; rEb = rex; rtm = rex
        SC2 = [A.alloc(f"SC{i}", [4, 2, 256], BF16) for i in range(2)]
        rsc = A.alloc("rsc", [3, 8], F32); reE2 = [A.alloc(f"reE{i}", [3, 8], F32) for i in range(2)]
        utkt2 = [A.alloc(f"utkt{i}", [2, 512], BF16) for i in range(2)]
        AU2 = [A.alloc(f"AU{i}", [8, 128], BF16) for i in range(2)]; AK2 = [A.alloc(f"AK{i}", [8, 128], BF16) for i in range(2)]
        TtF2 = [A.alloc(f"TtF{i}", [8, 64], BF16) for i in range(2)]
        Zb = [A.alloc(f"Zb{i}", [8, 64], BF16) for i in range(2)]
        Ztb = [A.alloc(f"Ztb{i}", [8, 64], BF16) for i in range(2)]
        Zbd = [A.alloc(f"Zbd{i}", [8, 128], BF16) for i in range(2)]
        Ztbd = [A.alloc(f"Ztbd{i}", [8, 128], BF16) for i in range(2)]
        for i in range(2):
            O.memset("pool", Zbd[i], 0.0); O.memset("pool", Ztbd[i], 0.0)
        Ttb = [A.alloc(f"Ttb{i}", [8, 64], BF16) for i in range(2)]
        RHSn = A.alloc("RHSn", [512], BF16); Nsb = A.alloc("Nsb", [512], BF16)
        Hr = A.alloc("Hr", [4, 64], F32); Hrp = A.alloc("Hrp", [4, 64], BF16); tmpH = A.alloc("tmpH", [4, 64], F32)
        yrw = [A.alloc(f"yrw{i}", [512], F32) for i in range(2)]
        if not (paired and d == "b"):
            O.memset("dve", Hr, 0.0)
        Hpm = [A.alloc(f"Hpm{i}", [4, 64], BF16) for i in range(2)]
        O.memset("pool", Hpm[0], 0.0); O.memset("pool", Hpm[1], 0.0)

        def v4(t):
            return t.v("p c (tc i) -> p c tc i", tc=2)


        def scalars(cs_, ex_, sc_, eE_, scale):
            c4, e4 = v4(cs_), v4(ex_)
            o = lambda k: sc_[:, k, :].v("p (c tc o) -> p c tc o", c=4, tc=2, o=1)
            O.tt("dve", o(0), c4[:, :, :, 63:64], e4[:, :, :, 0:1], ALU.subtract)
            if d == "f":
                O.tt("dve", o(1), c4[:, :, :, mloc:mloc + 1], e4[:, :, :, 0:1], ALU.subtract)
                O.tt("dve", o(2), c4[:, :, :, 63:64], c4[:, :, :, mloc:mloc + 1], ALU.subtract)
            else:
                O.tt("dve", o(1), c4[:, :, :, 63:64], e4[:, :, :, mloc:mloc + 1], ALU.subtract)
                O.tt("dve", o(2), e4[:, :, :, mloc:mloc + 1], e4[:, :, :, 0:1], ALU.subtract)
            O.act(eE_, sc_, AF.Exp, scale=scale)

        def esc(eE_, k, tc):
            return eE_[:, k, :].v("p (c tc) -> p c tc", tc=2)[:, :, tc:tc + 1]

        def rwkv_pre(ti, s, par):
            b = L[s]
            rw_, sw_, vr_ = b["rw"], b["sw"], b["vr"]
            SC, reE, utkt, AU, AK = SC2[par], reE2[par], utkt2[par], AU2[par], AK2[par]
            SCq = lambda q: SC[:, :, :, q * 64:(q + 1) * 64]
            sgn = -1.0 if d == "f" else 1.0
            fl = lambda t: t.v("p c t -> p (c t)")
            O.scan(fl(rcs), onesw, fl(sw_), 0.0, ALU.mult, ALU.add)
            O.tt("dve", rex, rcs, sw_, ALU.subtract)
            base = rcs if d == "f" else rex
            b4 = v4(base)
            O.tt("dve", v4(rRS), b4, b4[:, :, :, mloc:mloc + 1].bc([128, 4, 2, 64]), ALU.subtract)
            scalars(rcs, rex, rsc, reE, -DSC)
            O.act(rEa, rRS, AF.Exp, scale=sgn * DSC)
            O.tt("dve", SCq(3), v4(rw_[:, 0, :, :]), v4(rEa), ALU.mult)
            O.stt(rtm, rRS, sgn, sw_, ALU.mult, ALU.add)
            O.act(rEb, rtm, AF.Exp, scale=DSC)
            O.tt("dve", SCq(2), v4(rw_[:, 2, :, :]), v4(rEb), ALU.mult)
            yield
            O.act(rEa, rRS, AF.Exp, scale=-sgn * DSC)
            O.tt("dve", SCq(0), v4(rw_[:, 3, :, :]), v4(rEa), ALU.mult)
            O.tt("dve", SCq(1), v4(rw_[:, 1, :, :]), v4(rEa), ALU.mult)
            yield
            pT = PS[4].cast(BF16)
            for q in range(2):
                for c in range(4):
                    for tc in range(2):
                        O.tr(pT[tc * 64:(tc + 1) * 64, q * 512 + c * 128:q * 512 + (c + 1) * 128],
                             SC[:, c, tc, q * 64:(q + 1) * 64], identb)
            O.cp("act", utkt, pT.v("p (q x) -> p q x", q=2))
            yield
            bU = (PS[5], PS[6]); bX = (PS[7], PS[4])
            for c in range(4):
                for hh in range(2):
                    pr = slice(hh * 64, hh * 64 + 64)
                    for tc in range(2):
                        po = slice(tc * 64, tc * 64 + 64)
                        O.mm(bU[hh][po, c * 128:(c + 1) * 128], SC[pr, c, tc, 0:64], SC[pr, c, tc, 128:256], start=True, stop=True)
                        O.mm(bX[hh][po, c * 64:(c + 1) * 64], SC[pr, c, tc, 128:192], SC[pr, c, tc, 0:64], start=True, stop=True)
                if c % 2 == 1:
                    yield
            Zt_c = Ztb[0]
            for hh in range(2):
                O.tt("dve", AU[:, hh * 4:(hh + 1) * 4, :], bU[hh].v("p (h t) -> p h t", h=4),
                     cf("mu_" + d).us(1).bc([128, 4, 128]), ALU.mult)
                O.tt("dve", Zt_c[:, hh * 4:(hh + 1) * 4, :], bX[hh][:, 0:256].v("p (h t) -> p h t", h=4),
                     cf("mx_" + d, 64).us(1).bc([128, 4, 64]), ALU.mult)
            yield
            for c in range(4):
                for hh in range(2):
                    pr = slice(hh * 64, hh * 64 + 64)
                    for tc in range(2):
                        po = slice(tc * 64, tc * 64 + 64)
                        O.mm(bU[hh][po, c * 128:(c + 1) * 128], SC[pr, c, tc, 64:128], SC[pr, c, tc, 128:256], start=True, stop=True)
                if c % 2 == 1:
                    yield
            for hh in range(2):
                O.tt("dve", AK[:, hh * 4:(hh + 1) * 4, :], bU[hh].v("p (h t) -> p h t", h=4),
                     cf("mk_" + d).us(1).bc([128, 4, 128]), ALU.mult)
            Z_c = AU[:, :, 0:64]
            Tt_c = Ttb[0]
            O.tt("dve", Tt_c, Z_c, id2.us(1).bc([128, 8, 64]), ALU.add)
            yield
            def fill_bd(bd, src):
                O.cp("act", bd[0:64, :, 0:64], src[0:64])
                O.cp("dve", bd[64:128, :, 64:128], src[64:128])
            Zbd_c, Ztbd_c = Zbd[0], Ztbd[0]
            fill_bd(Zbd_c, Z_c)
            fill_bd(Ztbd_c, Zt_c)
            yield
            pZ, pZt, pTt = PS[4], PS[5], PS[7]
            for it in range(5):
                last = (it == 4)
                Zn, Ztn, Ttn = Zb[it % 2], Ztb[(it + 1) % 2], (TtF2[par] if last else Ttb[(it + 1) % 2])
                Zbd_n, Ztbd_n = Zbd[(it + 1) % 2], Ztbd[(it + 1) % 2]
                for j in range(8):
                    if not last:
                        O.mm(pZ[:, j * 64:(j + 1) * 64], Ztbd_c[:, j, :], Z_c[:, j, :], start=True, stop=True)
                    O.mm(pZt[:, j * 64:(j + 1) * 64], Zbd_c[:, j, :], Zt_c[:, j, :], start=True, stop=True)
                pZt3 = pZt.v("p (h t) -> p h t", h=8)
                O.cp("act", Ztbd_n[0:64, :, 0:64], pZt3[0:64])
                O.cp("dve", Ztbd_n[64:128, :, 64:128], pZt3[64:128])
                if not last:
                    pZ3 = pZ.v("p (h t) -> p h t", h=8)
                    O.cp("act", Ztn, pZt3)
                    O.cp("dve", Zn, pZ3)
                    O.cp("act", Zbd_n[0:64, :, 0:64], pZ3[0:64])
                    O.cp("dve", Zbd_n[64:128, :, 64:128], pZ3[64:128])
                yield
                for j in range(8):
                    O.mm(pTt[:, j * 64:(j + 1) * 64], Ztbd_n[:, j, :], Tt_c[:, j, :], start=True, stop=True)
                O.tt("dve", Ttn, pTt.v("p (h t) -> p h t", h=8), Tt_c, ALU.add)
                Z_c, Zt_c, Tt_c, Zbd_c, Ztbd_c = Zn, Ztn, Ttn, Zbd_n, Ztbd_n
                yield

        def rwkv_seq(ti, s, par):
            b = L[s]
            vr_ = b["vr"]
            SC, reE, utkt, AU, AK, Tt_c = SC2[par], reE2[par], utkt2[par], AU2[par], AK2[par], TtF2[par]
            yr_ = yrw[ti % 2]
            H3 = Hr
            for oi, tc in enumerate(order):
                po = slice(tc * 64, tc * 64 + 64)
                eAb = esc(reE, 1, tc).bc([128, 4, 64])
                O.tt("dve", Hpm[0][0:64], H3[0:64], eAb[0:64], ALU.mult)
                O.tt("dve", Hpm[1][64:128], H3[64:128], eAb[64:128], ALU.mult)
                bR = bN = PS[tc]
                for hd in range(8):
                    c, hh = hd // 2, hd % 2
                    j = hh * 4 + c
                    hs = slice(hd * 64, (hd + 1) * 64)
                    O.mm(bR[po, hs], SC[:, c, tc, 128:192], Hpm[hh][:, c, :], start=True, stop=False)
                    O.mm(bR[po, hs], AK[po, j, 0:64], vr_[po, hs], start=False, stop=True)
                O.ts("dve", RHSn[po], bR[po, :], -1.0, ALU.mult)
                yield
                for hd in range(8):
                    c, hh = hd // 2, hd % 2
                    j = hh * 4 + c
                    hs = slice(hd * 64, (hd + 1) * 64)
                    O.mm(bN[po, hs], Tt_c[po, j, :], RHSn[po, hs], start=True, stop=True)
                O.cp("act", Nsb[po], bN[po, :])
                yield
                for hd in range(8):
                    c, hh = hd // 2, hd % 2
                    j = hh * 4 + c
                    hs = slice(hd * 64, (hd + 1) * 64)
                    O.mm(PS[2][po, hs], SC[:, c, tc, 192:256], Hpm[hh][:, c, :], start=True, stop=False)
                    O.mm(PS[2][po, hs], AU[po, j, 64:128], Nsb[po, hs], start=False, stop=False)
                    O.mm(PS[2][po, hs], AK[po, j, 64:128], vr_[po, hs], start=False, stop=True)
                for hd in range(8):
                    c, hh = hd // 2, hd % 2
                    pr = slice(hh * 64, hh * 64 + 64)
                    hs = slice(hd * 64, (hd + 1) * 64)
                    O.mm(PS[3][pr, c * 64:(c + 1) * 64], utkt[po, 0, hs], Nsb[po, hs], start=True, stop=False)
                    O.mm(PS[3][pr, c * 64:(c + 1) * 64], utkt[po, 1, hs], vr_[po, hs], start=False, stop=True)
                O.tt("dve", tmpH, PS[3][:, 0:256].v("p (c v) -> p c v", c=4), esc(reE, 2, tc).bc([128, 4, 64]), ALU.mult)
                O.tt("dve", H3, H3, esc(reE, 0, tc).bc([128, 4, 64]), ALU.mult)
                O.tt("dve", H3, H3, tmpH, ALU.add)
                yield
            O.cp("act", yr_, PS[2][:, 0:512])
            O.dma("pool", f"syr{ti % 2}", [(DR(YR[di, ti * 128:(ti + 1) * 128, :]), yr_)])
            yield

        hcs = A.alloc("hcs", [4, 128], F32); hex_ = A.alloc("hex", [4, 128], F32)
        hRS = A.alloc("hRS", [4, 128], F32)
        hEa = hcs; hEb = hex_
        qt = A.alloc("qt", [4, 128], BF16); kt = A.alloc("kt", [4, 128], BF16)
        hsc = A.alloc("hsc", [3, 8], F32); heE = A.alloc("heE", [3, 8], F32)
        kttm = A.alloc("kttm", [512], BF16)
        Sm = A.alloc("Sm", [4, 128], BF16)
        Hs = A.alloc("Hs", [4, 128], F32); Hsp = A.alloc("Hsp", [4, 128], BF16); tmpS = A.alloc("tmpS", [4, 128], F32)
        yhw = [A.alloc(f"yhw{i}", [512], F32) for i in range(2)]
        if not (paired and d == "b"):
            O.memset("dve", Hs, 0.0)
        if paired and d == "b":
            exr = A.alloc("exr", [1792], F32); exs = A.alloc("exs", [1792], F32)
            O.dma("sp", "lex", [(exr, DR(EXr[:, :], "EXr")), (exs, DR(EXs[:, :], "EXs"))])
            O.tt("dve", Sf.v("p g x -> p (g x)"), exr[:, 0:1024], exs[:, 0:1024], ALU.subtract)
            O.cp("act", Sbf, Sf)
            O.tt("dve", Hr.v("p c v -> p (c v)"), exr[:, 1024:1280], exs[:, 1024:1280], ALU.subtract)
            O.tt("dve", Hs.v("p c v -> p (c v)"), exr[:, 1280:1792], exs[:, 1280:1792], ALU.subtract)

        def hgrn_tile(ti, s):
            b = L[s]
            qh_, kh_, lf_, ih_ = b["qh"], b["kh"], b["lf"], b["ih"]
            sgn = 1.0 if d == "f" else -1.0
            fl = lambda t: t.v("p c t -> p (c t)")
            O.scan(fl(hcs), onesw, fl(lf_), 0.0, ALU.mult, ALU.add)
            O.tt("dve", hex_, hcs, lf_, ALU.subtract)
            base = hcs if d == "f" else hex_
            b4 = v4(base)
            O.tt("dve", v4(hRS), b4, b4[:, :, :, mloc:mloc + 1].bc([128, 4, 2, 64]), ALU.subtract)
            scalars(hcs, hex_, hsc, heE, 1.0)
            O.act(hEa, hRS, AF.Exp, scale=sgn)
            O.tt("dve", qt, qh_, hEa, ALU.mult)
            O.act(hEb, hRS, AF.Exp, scale=-sgn)
            O.tt("dve", kt, kh_, hEb, ALU.mult)
            yield
            pT = PS[4].cast(BF16)
            for c in range(4):
                O.tr(pT[:, c * 128:(c + 1) * 128], kt[:, c, :], identb)
            O.cp("act", kttm, pT[:, 0:512])
            for c in range(4):
                O.mm(PS[5][:, c * 128:(c + 1) * 128], kt[:, c, :], qt[:, c, :], start=True, stop=True)
            O.tt("dve", Sm, PS[5].v("p (c t) -> p c t", c=4), cf("mh_" + d).us(1).bc([128, 4, 128]), ALU.mult)
            yield
            for tc in range(2):
                po = slice(tc * 64, tc * 64 + 64)
                for c in range(4):
                    O.mm(PS[6 + tc][:, c * 128:(c + 1) * 128], kttm[po, c * 128:(c + 1) * 128], ih_[po, c * 128:(c + 1) * 128],
                         start=True, stop=True)
            for c in range(4):
                O.mm(PS[4][:, c * 128:(c + 1) * 128], Sm[:, c, :], ih_[:, c * 128:(c + 1) * 128], start=(c == 0), stop=False)
            yield
            for oi, tc in enumerate(order):
                po = slice(tc * 64, tc * 64 + 64)
                O.tt("dve", Hsp, Hs, esc(heE, 1, tc).bc([128, 4, 128]), ALU.mult)
                for c in range(4):
                    O.mm(PS[4][po, c * 128:(c + 1) * 128], qt[:, c, tc * 64:(tc + 1) * 64], Hsp[:, c, :], start=False,
                         stop=(oi == 1))
                O.tt("dve", tmpS, PS[6 + tc].v("p (c v) -> p c v", c=4), esc(heE, 2, tc).bc([128, 4, 128]), ALU.mult)
                O.tt("dve", Hs, Hs, esc(heE, 0, tc).bc([128, 4, 128]), ALU.mult)
                O.tt("dve", Hs, Hs, tmpS, ALU.add)
                yield
            yh_ = yhw[ti % 2]
            O.cp("act", yh_, PS[4][:, 0:512])
            O.dma("pool", f"syh{ti % 2}", [(DR(YH[di, ti * 128:(ti + 1) * 128, :]), yh_)])
            yield

        branches = [b_ for b_ in ("ssd", "rwkv", "hgrn") if b_ not in skip]

        def run_streams(gens):
            if interleave:
                while gens:
                    for g_ in list(gens):
                        try:
                            next(g_)
                        except StopIteration:
                            gens.remove(g_)
            else:
                for g_ in gens:
                    for _ in g_:
                        pass

        load(tiles[0], 0)
        if "rwkv" in branches:
            run_streams([rwkv_pre(tiles[0], 0, 0)])
        for n, ti in enumerate(tiles):
            s = n % 2
            if n + 1 < len(tiles):
                load(tiles[n + 1], 1 - s)

            def streamA():
                if "ssd" in branches:
                    yield from ssd_tile(ti, s)
                if "rwkv" in branches:
                    yield from rwkv_seq(ti, s, n % 2)

            def streamB():
                if "hgrn" in branches:
                    yield from hgrn_tile(ti, s)
                if "rwkv" in branches and n + 1 < len(tiles):
                    yield from rwkv_pre(tiles[n + 1], 1 - s, (n + 1) % 2)
            run_streams([streamA(), streamB()])
        if paired and d == "f":
            O.dma("pool", "sex", [(DR(EXs[:, 0:1024], "EXs"), Sf.v("p g x -> p (g x)")),
                                  (DR(EXs[:, 1024:1280], "EXs"), Hr.v("p c v -> p (c v)")),
                                  (DR(EXs[:, 1280:1792], "EXs"), Hs.v("p c v -> p (c v)"))])
            P.barrier()
            P.cc("pool", lambda e: e.collective_compute("AllReduce", ALU.add, replica_groups=groups,
                                                         ins=[EXs.opt()], outs=[EXr.opt()]), f"ccs{l}", ["EXs"], ["EXr"])
        P.barrier()
        A.release(m)

    def head_rstd(ssv, n, scale, eps):
        O.act(ssv, ssv, AF.Ln, bias=eps, scale=scale)
        O.act(ssv, ssv, AF.Exp, scale=-0.5)

    def phaseE(l, xsrc):
        m = A.mark()
        wbs = A.alloc("wbs", [2, 8, 512], BF16); wbr = A.alloc("wbr", [2, 4, 512], BF16)
        wbh = A.alloc("wbh", [2, 4, 512], BF16); wo = A.alloc("wo", [2, 8, 512], BF16)
        O.dma("sp", "lwE", [(wbs[:, cbk], DR(WBS[l, cbk])) for cbk in range(2)] + [(wbr[:, cbk], DR(WBR[l, cbk])) for cbk in range(2)] +
              [(wbh[:, cbk], DR(WBH[l, cbk])) for cbk in range(2)] + [(wo[:, cbk], DR(WO[l, cbk])) for cbk in range(2)])
        xs2 = [A.alloc(f"exs{i}", [1024], BF16) for i in range(2)]; zs2 = [A.alloc(f"ezs{i}", [1024], BF16) for i in range(2)]
        ys2 = [A.alloc(f"eys{i}", [2, 1024], F32) for i in range(2)]
        yr_ = A.alloc("eyr", [2, 512], F32); yh_ = A.alloc("eyh", [2, 512], F32)
        vr_ = A.alloc("evr", [512], BF16); gr_ = A.alloc("egr", [512], BF16); gh_ = A.alloc("egh", [512], BF16)
        bc_ = A.alloc("ebc", [8], F32); gts_ = A.alloc("egts", [3072], BF16); xr_ = A.alloc("exr", [1024], F32)
        y = A.alloc("ey", [1024], F32); t = A.alloc("et", [1024], F32)
        sq = A.alloc("esq", [1024], BF16); ss = A.alloc("ess", [16], F32)
        ss_s = A.alloc("ess_s", [1], F32); ss_h = A.alloc("ess_h", [4], F32)
        ynb = A.alloc("eynb", [1024], BF16); ynT = A.alloc("eynT", [8, 128], BF16)
        ry = A.alloc("ery", [512], F32); rt = A.alloc("ert", [512], F32)
        rnb = A.alloc("ernb", [512], BF16); rnT = A.alloc("ernT", [4, 128], BF16)
        hy = A.alloc("ehy", [512], F32); ht = A.alloc("eht", [512], F32)
        hnb = A.alloc("ehnb", [512], BF16); hnT = A.alloc("ehnT", [4, 128], BF16)
        m1 = A.alloc("em1", [512], F32); m2 = A.alloc("em2", [512], F32)
        mixb = A.alloc("emixb", [1024], BF16); mT = A.alloc("emT", [8, 128], BF16)
        xo = [A.alloc(f"exo{i}", [1024], F32) for i in range(2)]
        def loadE1(ti):
            tsl = slice(ti * 128, (ti + 1) * 128)
            k = ti % 2
            O.dma("sp", f"lE1{k}", [(xs2[k], DR(XS[tsl, :])), (zs2[k], DR(ZS[tsl, :])), (ys2[k][:, 0, :], DR(YS[0, tsl, :])), (ys2[k][:, 1, :], DR(YS[1, tsl, :]))])

        def loadE2(ti):
            tsl = slice(ti * 128, (ti + 1) * 128)
            O.dma("sp", "lE2", [(yr_[:, 0, :], DR(YR[0, tsl, :])), (yr_[:, 1, :], DR(YR[1, tsl, :])), (vr_, DR(VR[tsl, :])), (gr_, DR(GR[tsl, :])), (bc_, DR(BC[tsl, :]))])
            O.dma("sp", "lE3", [(yh_[:, 0, :], DR(YH[0, tsl, :])), (yh_[:, 1, :], DR(YH[1, tsl, :])), (gh_, DR(GH[tsl, :]))])
            O.dma("sp", "lE4", [(gts_, DR(GTs[tsl, :])), (xr_, DR(xsrc[tsl, :]))])

        loadE1(0)
        loadE2(0)
        for ti in range(NT):
            tsl = slice(ti * 128, (ti + 1) * 128)
            xs_, zs_, ys_ = xs2[ti % 2], zs2[ti % 2], ys2[ti % 2]
            if ti + 1 < NT:
                loadE1(ti + 1)
            def g_ssd():
                O.tt("pool", y, ys_[:, 0, :], ys_[:, 1, :], ALU.add)
                yield
                O.tt("dve", t.v("p (h e) -> p h e", h=16), xs_.v("p (h e) -> p h e", h=16), dskip.us(2).bc([128, 16, 64]), ALU.mult)
                O.tt("dve", y, y, t, ALU.add)
                yield
                O.tt("dve", y, y, zs_, ALU.mult)
                O.act(sq, y, AF.Square, accum=ss_s[:, 0:1])
                yield
                head_rstd(ss_s[:, 0:1], 1, 1.0 / 1024, EPS)
                O.stt(ynb, y, ss_s[:, 0:1], snw, ALU.mult, ALU.mult)
                yield
                pT = PS[0].cast(BF16)
                for kc in range(8):
                    O.tr(pT[:, kc * 128:(kc + 1) * 128], ynb[:, kc * 128:(kc + 1) * 128], identb)
                O.cp("act", ynT, pT.v("p (k t) -> p k t", k=8))
                yield
                yield
            def g_rw():
                ry3 = ry.v("p (h e) -> p h e", h=8); rt3 = rt.v("p (h e) -> p h e", h=8)
                yield
                O.tt("pool", ry, yr_[:, 0, :], yr_[:, 1, :], ALU.add)
                P.op("dve", lambda e: e.tensor_reduce(out=ss[:, 0:8].ap, in_=ry3.ap, axis=AX.X, op=ALU.add), [ry.key], [ss.key])
                yield
                O.ts("dve", ss[:, 0:8], ss[:, 0:8], 1.0 / 64, ALU.mult)
                O.tt("dve", ry3, ry3, ss[:, 0:8].us(2).bc([128, 8, 64]), ALU.subtract)
                yield
                O.tt("dve", rt, ry, ry, ALU.mult)
                P.op("dve", lambda e: e.tensor_reduce(out=ss[:, 8:16].ap, in_=rt3.ap, axis=AX.X, op=ALU.add), [rt.key], [ss.key])
                yield
                head_rstd(ss[:, 8:16], 8, 1.0 / 64, 64e-5)
                O.tt("dve", ry3, ry3, ss[:, 8:16].us(2).bc([128, 8, 64]), ALU.mult)
                yield
                O.tt("pool", ry, ry, lnw, ALU.mult)
                O.tt("dve", ry, ry, lnb, ALU.add)
                yield
                O.tt("dve", rt3, vr_.v("p (h e) -> p h e", h=8), bc_.us(2).bc([128, 8, 64]), ALU.mult)
                O.tt("dve", ry, ry, rt, ALU.add)
                yield
                O.tt("dve", rnb, ry, gr_, ALU.mult)
                pT = PS[1].cast(BF16)
                yield
                for kc in range(4):
                    O.tr(pT[:, kc * 128:(kc + 1) * 128], rnb[:, kc * 128:(kc + 1) * 128], identb)
                    yield
                O.cp("act", rnT, pT[:, 0:512].v("p (k t) -> p k t", k=4))
                yield
            def g_hg():
                hy3 = hy.v("p (h e) -> p h e", h=4); ht3 = ht.v("p (h e) -> p h e", h=4)
                yield
                O.tt("pool", hy, yh_[:, 0, :], yh_[:, 1, :], ALU.add)
                O.tt("dve", ht, hy, hy, ALU.mult)
                yield
                P.op("dve", lambda e: e.tensor_reduce(out=ss_h[:, 0:4].ap, in_=ht3.ap, axis=AX.X, op=ALU.add), [ht.key], [ss_h.key])
                head_rstd(ss_h[:, 0:4], 4, 1.0 / 128, EPS)
                yield
                O.tt("dve", hy3, hy3, ss_h[:, 0:4].us(2).bc([128, 4, 128]), ALU.mult)
                O.tt("pool", hy, hy, hnw, ALU.mult)
                yield
                O.tt("dve", hnb, hy, gh_, ALU.mult)
                pT = PS[2].cast(BF16)
                yield
                for kc in range(4):
                    O.tr(pT[:, kc * 128:(kc + 1) * 128], hnb[:, kc * 128:(kc + 1) * 128], identb)
                    yield
                O.cp("act", hnT, pT[:, 0:512].v("p (k t) -> p k t", k=4))
                yield
            _g = [g_ssd(), g_rw(), g_hg()]
            while _g:
                for _x in list(_g):
                    try:
                        next(_x)
                    except StopIteration:
                        _g.remove(_x)
            for cbk in range(2):
                cs_ = slice(cbk * 512, (cbk + 1) * 512)
                for kc in range(8):
                    O.mm(PS[3][:, 0:512], ynT[:, kc, :], wbs[:, cbk, kc, :], start=(kc == 0), stop=(kc == 7))
                for kc in range(4):
                    O.mm(PS[4][:, 0:512], rnT[:, kc, :], wbr[:, cbk, kc, :], start=(kc == 0), stop=(kc == 3))
                for kc in range(4):
                    O.mm(PS[5][:, 0:512], hnT[:, kc, :], wbh[:, cbk, kc, :], start=(kc == 0), stop=(kc == 3))
                O.tt("dve", m1, PS[3][:, 0:512], gts_[:, cbk * 512:cbk * 512 + 512], ALU.mult)
                O.tt("dve", m2, PS[4][:, 0:512], gts_[:, 1024 + cbk * 512:1024 + cbk * 512 + 512], ALU.mult)
                O.tt("dve", m1, m1, m2, ALU.add)
                O.tt("dve", m2, PS[5][:, 0:512], gts_[:, 2048 + cbk * 512:2048 + cbk * 512 + 512], ALU.mult)
                O.tt("pool", mixb[:, cs_], m1, m2, ALU.add)
            pT = PS[6].cast(BF16)
            for kc in range(8):
                O.tr(pT[:, kc * 128:(kc + 1) * 128], mixb[:, kc * 128:(kc + 1) * 128], identb)
            O.cp("act", mT, pT.v("p (k t) -> p k t", k=8))
            xo_ = xo[ti % 2]
            for cbk in range(2):
                cs_ = slice(cbk * 512, (cbk + 1) * 512)
                for kc in range(8):
                    O.mm(PS[7][:, 0:512], mT[:, kc, :], wo[:, cbk, kc, :], start=(kc == 0), stop=(kc == 7))
                O.tt("dve", xo_[:, cs_], PS[7][:, 0:512], xr_[:, cs_], ALU.add)
            O.dma("pool", f"sxo{ti % 2}", [(DR(XM[tsl, :]), xo_)])
            if ti + 1 < NT:
                loadE2(ti + 1)
        P.barrier()
        A.release(m)

    def phaseD(l, last):
        m = A.mark()
        fdn = A.alloc("fdn", [2, 22, 512], BF16)
        O.dma("sp", "lwD", [(fdn[:, cbk], DR(FDN[l, cbk])) for cbk in range(2)])
        xin = [A.alloc(f"dxin{i}", [D], F32) for i in range(2)]
        sq = A.alloc("dsq", [D], BF16); ss = A.alloc("dss", [1], F32)
        xn = [A.alloc(f"dxn{i}", [D], BF16) for i in range(2)]
        xnT = A.alloc("dxnT", [8, G], BF16)
        wst = [A.alloc(f"dwst{i}", [8, 512], BF16) for i in range(3)]
        actT = A.alloc("actT", [22, G], BF16)
        sgt = [A.alloc(f"sgt{i}", [G], F32) for i in range(2)]
        xres = A.alloc("xres", [D], F32)
        xo = A.alloc("dxo", [D], F32)
        fo = A.alloc("dfo", [D], F32)
        wctr = [0]

        def load_w(bi):
            w = wst[wctr[0] % 3]
            O.dma("sp", f"ldwD{wctr[0] % 3}", [(w, DR(FIN[l, bi]))])
            wctr[0] += 1
            return w

        for gi in range(NG):
            t0 = gi * G
            for j in range(GT):
                xi = xin[j % 2]
                O.dma("sp", f"ldxD{j % 2}", [(xi, DR(XM[t0 + j * 128:t0 + (j + 1) * 128, :]))])
                psb = norm_T(xi, 128, n2w, xn[j % 2], sq, ss, None, PS[0])
                O.cp(rr(["act", "dve"]), xnT[:, :, j * 128:(j + 1) * 128], psb.v("p (k t) -> p k t", k=8))
            for b in range(6):
                wg = load_w(b)
                wu = load_w(6 + b)
                nch = 4 if b < 5 else 2
                for c in range(nch):
                    jj = b * 4 + c
                    pg, pu = PS[1 + (jj % 2) * 2], PS[2 + (jj % 2) * 2]
                    for kc in range(8):
                        O.mm(pg[:, 0:G], wg[:, kc, c * 128:(c + 1) * 128], xnT[:, kc, :], start=(kc == 0), stop=(kc == 7))
                    for kc in range(8):
                        O.mm(pu[:, 0:G], wu[:, kc, c * 128:(c + 1) * 128], xnT[:, kc, :], start=(kc == 0), stop=(kc == 7))
                    sg_ = sgt[jj % 2]
                    O.act(sg_, pg[:, 0:G], AF.Silu)
                    O.tt("dve", actT[:, jj, :], pu[:, 0:G], sg_, ALU.mult)
            for j in range(GT):
                tsl = slice(t0 + j * 128, t0 + (j + 1) * 128)
                O.dma("sp", "ldxr", [(xres, DR(XM[tsl, :]))])
                for cbk in range(2):
                    pd = PS[5 + cbk]
                    for jj in range(22):
                        O.mm(pd[:, 0:512], actT[:, jj, j * 128:(j + 1) * 128], fdn[:, cbk, jj, :], start=(jj == 0), stop=(jj == 21))
                    O.tt("dve", xo[:, cbk * 512:(cbk + 1) * 512], pd[:, 0:512], xres[:, cbk * 512:(cbk + 1) * 512], ALU.add)
                if not last:
                    O.dma("pool", "sxD", [(DR(X1[tsl, :]), xo)])
                else:
                    O.act(sq, xo, AF.Square, accum=ss)
                    head_rstd(ss, 1, 1.0 / D, EPS)
                    O.stt(fo, xo, ss[:, 0:1], fnwb, ALU.mult, ALU.mult)
                    O.dma("pool", "sfo", [(DR(out_d[tsl, :]), fo)])
        P.barrier()
        A.release(m)

    def halo_exchange(l):
        m = A.mark()
        O.dma("pool", "shx", [(DR(HX.rearrange("(r a) b -> r (a b)", r=2), "HX"), DR(X1[T - 2:T, :]))])
        P.barrier()
        P.cc("pool", lambda e: e.collective_compute("AllReduce", ALU.add, replica_groups=groups,
                                                     ins=[HX.opt()], outs=[HXr.opt()]), f"cch{l}", ["HX"], ["HXr"])
        P.barrier()
        h1 = A.alloc("h1", [D], F32); h2 = A.alloc("h2", [D], F32)
        O.dma("sp", "lhx", [(h1[0:2], DR(HXr.rearrange("(r a) b -> r (a b)", r=2), "HXr")),
                            (h2[0:2], DR(HX.rearrange("(r a) b -> r (a b)", r=2), "HX"))])
        O.tt("dve", h1[0:2], h1[0:2], h2[0:2], ALU.subtract)
        O.dma("pool", "shd", [(DR(HD[0:1, :], "HD"), h1[1:2]), (DR(HD[1:2, :], "HD"), h1[0:1])])
        P.barrier()
        A.release(m)

    prologue()
    for l in range(depth):
        layer_setup(l)
        xsrc = x_in if l == 0 else X1
        rh = None
        if paired:
            rh = xhalo if l == 0 else HD
        phaseA(l, xsrc, rh)
        if stop == "A":
            break
        scan_phase(l, "f")
        scan_phase(l, "b")
        if stop == "C":
            break
        phaseE(l, xsrc)
        if stop == "E":
            break
        phaseD(l, l == depth - 1)
        if paired and l < depth - 1:
            halo_exchange(l)
    P.barrier()
    P.emit()
    es.close()
    return nc


def pack_params(inp, depth):
    f = lambda a: np.asarray(a, np.float32)
    colp = np.zeros((depth, 128, NCOLP), np.float32)
    rowp = np.zeros((depth, 1, NROWP), np.float32)
    lbl = f(inp["hgrn_lb_logits"])
    for l in range(depth):
        cwv = f(inp["ssm_conv_w"])[l, :, 0, :]
        colp[l, :, 0:60] = cwv.reshape(5, 12, 128).transpose(2, 1, 0).reshape(128, 60)
        colp[l, :, 60:72] = f(inp["ssm_conv_b"])[l].reshape(12, 128).T
        colp[l, :, 72:100] = f(inp["rwkv_mu"])[l].reshape(2, 14, 128).transpose(2, 1, 0).reshape(128, 28)
        colp[l, :, 100:108] = f(inp["rwkv_w0"])[l].reshape(2, 4, 128).transpose(2, 0, 1).reshape(128, 8)
        colp[l, :, 108:112] = f(inp["rwkv_a0"])[l].reshape(4, 128).T
        colp[l, :, 112:116] = f(inp["rwkv_k_k"])[l].reshape(4, 128).T
        colp[l, :, 116:120] = f(inp["rwkv_k_a"])[l].reshape(4, 128).T
        colp[l, :, 120:124] = f(inp["rwkv_r_k"])[l].reshape(4, 128).T
        for l2 in range(min(depth, 2)):
            colp[l, :, 124 + 4 * l2:128 + 4 * l2] = lbl[l2].reshape(4, 128).T
        rowp[l, 0, 0:1024] = f(inp["norm1_w"])[l]
        rowp[l, 0, 1024:2048] = f(inp["norm2_w"])[l]
        rowp[l, 0, 2048:3072] = f(inp["ssm_norm_w"])[l]
        rowp[l, 0, 3072:3584] = f(inp["rwkv_ln_w"])[l]
        rowp[l, 0, 3584:4096] = f(inp["rwkv_ln_b"])[l]
        rowp[l, 0, 4096:4608] = f(inp["hgrn_norm_w"])[l]
        rowp[l, 0, 4608:4640] = f(inp["ssm_dt_bias"])[l].reshape(32)
        rowp[l, 0, 4640:4672] = f(inp["ssm_a_log"])[l].reshape(32)
        rowp[l, 0, 4672:4688] = f(inp["ssm_d"])[l]
    return colp, rowp


def make_in_map(inp, xs, depth):
    f = lambda a: np.ascontiguousarray(np.asarray(a, np.float32))
    colp, rowp = pack_params(inp, depth)
    m = {"x": f(xs), "colp": colp, "rowp": rowp, "fnw": f(inp["final_norm_w"]).reshape(1, D), "consts": CONSTS}
    for k in ("w_in", "w_branch_ssm", "w_branch_rwkv", "w_branch_hgrn", "w_out", "ffn_w_in", "ffn_w_down",
              "rwkv_w_up", "rwkv_a_up", "rwkv_g_up"):
        m[k] = f(inp[k])[:depth]
    return m


def flip_params(inp):
    o = dict(inp)
    w = np.array(inp["w_in"], np.float32, copy=True)
    w[:, :, 2560:2576], w[:, :, 2576:2592] = np.array(inp["w_in"])[:, :, 2576:2592], np.array(inp["w_in"])[:, :, 2560:2576]
    w[:, :, 4896:5408], w[:, :, 5408:5920] = np.array(inp["w_in"])[:, :, 5408:5920], np.array(inp["w_in"])[:, :, 4896:5408]
    o["w_in"] = w
    o["ssm_conv_w"] = np.asarray(inp["ssm_conv_w"])[:, ::-1]
    o["ssm_dt_bias"] = np.asarray(inp["ssm_dt_bias"])[:, ::-1]
    o["ssm_a_log"] = np.asarray(inp["ssm_a_log"])[:, ::-1]
    o["rwkv_mu"] = np.asarray(inp["rwkv_mu"])[:, ::-1]
    o["rwkv_w0"] = np.asarray(inp["rwkv_w0"])[:, ::-1]
    o["rwkv_w_up"] = np.asarray(inp["rwkv_w_up"])[:, ::-1]
    return o


def make_maps_paired(inputs, depth):
    x = np.asarray(inputs["x"], np.float32)
    B, L, _ = x.shape
    H = L // 2
    finp = flip_params(inputs)
    maps = []
    for b in range(B):
        ma = make_in_map(inputs, x[b, :H], depth)
        ma["xhalo"] = np.ascontiguousarray(x[b, H:H + 2])
        mb = make_in_map(finp, x[b, H:][::-1], depth)
        mb["xhalo"] = np.ascontiguousarray(x[b, H - 2:H][::-1])
        maps += [ma, mb]
    return maps


def kernel(**inputs):
    x = np.ascontiguousarray(np.asarray(inputs["x"], np.float32))
    B, L, _ = x.shape
    depth = int(np.asarray(inputs["w_in"]).shape[0])
    nc = build(L // 2, depth, npairs=B)
    maps = make_maps_paired(inputs, depth)
    res = run_bass_kernel_spmd(nc, maps, core_ids=list(range(2 * B)))
    outs = []
    for b in range(B):
        oa = np.asarray(res.results[2 * b]["out"], np.float32)
        ob = np.asarray(res.results[2 * b + 1]["out"], np.float32)[::-1]
        outs.append(np.concatenate([oa, ob], 0))
    return np.stack(outs).astype(np.float32)
```

```python
import contextlib
import numpy as np
import concourse.bass as bass
import concourse.mybir as mybir
from concourse.bass_utils import run_bass_kernel_spmd

F32 = mybir.dt.float32
BF16 = mybir.dt.bfloat16
AF = mybir.ActivationFunctionType
ALU = mybir.AluOpType
AX = mybir.AxisListType

ENGS = ("pe", "act", "dve", "pool", "sp")


class Prog:
    def __init__(self, nc):
        self.nc = nc
        self.q = {e: [] for e in ENGS}
        self.last_w = {}
        self.readers = {}
        self.dma_cnt = {}
        self.seen = {e: {} for e in ENGS}
        self.nwaits = 0

    def _deps(self, eng, reads, writes):
        deps = set()
        idx = len(self.q[eng])
        for k in reads:
            t = self.last_w.get(k)
            if t is not None:
                if t[0] == eng:
                    if eng == "pe" or t[1] < idx - 1:
                        continue
                deps.add(t)
        for k in writes:
            t = self.last_w.get(k)
            if t is not None and t[0] != eng:
                deps.add(t)
            for t in self.readers.get(k, ()):
                if t[0] != eng:
                    deps.add(t)
        out = []
        seen = self.seen[eng]
        best = {}
        for (sk, v) in deps:
            if seen.get(sk, 0) >= v:
                continue
            if best.get(sk, 0) < v:
                best[sk] = v
        for sk, v in best.items():
            seen[sk] = v
            out.append((sk, v))
        return out

    def op(self, eng, fn, reads=(), writes=()):
        deps = self._deps(eng, reads, writes)
        idx = len(self.q[eng])
        self.q[eng].append(dict(fn=fn, deps=deps, dma=None))
        tok = (eng, idx + 1)
        for k in reads:
            self.readers.setdefault(k, []).append(tok)
        for k in writes:
            self.last_w[k] = tok
            self.readers[k] = []
        return tok

    def dma(self, eng, fns, sem, reads=(), writes=()):
        deps = self._deps(eng, reads, writes)
        n0 = self.dma_cnt.get(sem, 0)
        n1 = n0 + len(fns)
        self.dma_cnt[sem] = n1
        for i, fn in enumerate(fns):
            self.q[eng].append(dict(fn=fn, deps=deps if i == 0 else [], dma=sem))
        tok = (("dma", sem), 16 * n1)
        for k in reads:
            self.readers.setdefault(k, []).append(tok)
        for k in writes:
            self.last_w[k] = tok
            self.readers[k] = []
        return tok

    def cc(self, eng, fn, sem, reads=(), writes=()):
        deps = self._deps(eng, reads, writes)
        assert sem not in self.dma_cnt
        self.dma_cnt[sem] = 0
        self.cc_sems = getattr(self, "cc_sems", set()); self.cc_sems.add(sem)
        self.q[eng].append(dict(fn=fn, deps=deps, dma=sem, inc=None))
        tok = (("dma", sem), 1)
        for k in reads:
            self.readers.setdefault(k, []).append(tok)
        for k in writes:
            self.last_w[k] = tok
            self.readers[k] = []
        return tok

    def barrier(self):
        toks = []
        for e in ENGS:
            last = 0
            for i, rec in enumerate(self.q[e]):
                if rec["fn"] is not None and rec["dma"] is None:
                    last = i + 1
            if last:
                toks.append((e, last))
        for s_, n in self.dma_cnt.items():
            if s_ in getattr(self, "cc_sems", ()):
                toks.append((("dma", s_), 1))
            elif n:
                toks.append((("dma", s_), 16 * n))
        for e in ENGS:
            seen = self.seen[e]
            deps = []
            for (sk, v) in toks:
                if sk == e or seen.get(sk, 0) >= v:
                    continue
                seen[sk] = v
                deps.append((sk, v))
            self.q[e].append(dict(fn=None, deps=deps, dma=None))
        self.last_w = {}
        self.readers = {}

    def wait_all(self, eng, keys):
        deps = self._deps(eng, (), keys)
        self.q[eng].append(dict(fn=None, deps=deps, dma=None))

    def emit(self):
        nc = self.nc
        signaled = {e: set() for e in ENGS}
        for e in ENGS:
            for rec in self.q[e]:
                for (sk, v) in rec["deps"]:
                    if sk in ENGS:
                        signaled[sk].add(v)
        valmap = {}
        for e in ENGS:
            s = sorted(signaled[e])
            valmap[e] = {v: i + 1 for i, v in enumerate(s)}
        dma_sems = sorted(self.dma_cnt.keys())
        with contextlib.ExitStack() as es:
            esem = {e: es.enter_context(nc.semaphore(f"sem_{e}")) for e in ENGS}
            dsem = {s: es.enter_context(nc.semaphore(f"dsem_{s}")) for s in dma_sems}
            block = es.enter_context(nc.Block())
            engmap = {"pe": "tensor", "act": "scalar", "dve": "vector", "pool": "gpsimd", "sp": "sync"}

            def make(e):
                def body(engobj):
                    for i, rec in enumerate(self.q[e]):
                        for (sk, v) in rec["deps"]:
                            if sk in ENGS:
                                engobj.wait_ge(esem[sk], valmap[sk][v])
                            else:
                                engobj.wait_ge(dsem[sk[1]], v)
                            self.nwaits += 1
                        if rec["fn"] is None:
                            continue
                        ins = rec["fn"](engobj)
                        if rec["dma"] is not None:
                            if rec.get("inc", 16) is None:
                                ins.then_inc(dsem[rec["dma"]])
                            else:
                                ins.then_inc(dsem[rec["dma"]], 16)
                        elif (i + 1) in valmap[e]:
                            ins.then_inc(esem[e], 1)
                return body

            for e in ENGS:
                if self.q[e]:
                    getattr(block, engmap[e])(make(e))


class T_:
    __slots__ = ("ap", "key", "psum")

    def __init__(self, ap, key, psum=False):
        self.ap = ap; self.key = key; self.psum = psum

    def __getitem__(self, idx):
        return T_(self.ap[idx], self.key, self.psum)

    def v(self, pat, **kw):
        return T_(self.ap.rearrange(pat, **kw), self.key, self.psum)

    def bc(self, shape):
        return T_(self.ap.to_broadcast(list(shape)), self.key, self.psum)

    def us(self, axis):
        return T_(self.ap.unsqueeze(axis), self.key, self.psum)

    def cast(self, dt):
        return T_(self.ap.bitcast(dt), self.key, self.psum)

    @property
    def shape(self):
        return self.ap.shape


def _rw(outs, ins):
    r, w = [], []
    for t in ins:
        if isinstance(t, T_):
            (w if t.psum else r).append(t.key)
    for t in outs:
        if isinstance(t, T_):
            w.append(t.key)
    return r, w


def _a(x):
    return x.ap if isinstance(x, T_) else x


class Ops:
    def __init__(self, P):
        self.P = P

    def act(self, out, in_, func, bias=0.0, scale=1.0, accum=None):
        r, w = _rw([out, accum], [in_, bias, scale])
        kw = {}
        if accum is not None:
            kw["accum_out"] = _a(accum)
        self.P.op("act", lambda e: e.activation(out=_a(out), in_=_a(in_), func=func, bias=_a(bias), scale=_a(scale), **kw), r, w)

    def tt(self, eng, out, a, b, op):
        r, w = _rw([out], [a, b])
        self.P.op(eng, lambda e: e.tensor_tensor(out=_a(out), in0=_a(a), in1=_a(b), op=op), r, w)

    def ts(self, eng, out, a, s1, op0, s2=None, op1=None):
        r, w = _rw([out], [a, s1, s2])
        if op1 is None:
            self.P.op(eng, lambda e: e.tensor_scalar(out=_a(out), in0=_a(a), scalar1=_a(s1), scalar2=None, op0=op0), r, w)
        else:
            self.P.op(eng, lambda e: e.tensor_scalar(out=_a(out), in0=_a(a), scalar1=_a(s1), scalar2=_a(s2), op0=op0, op1=op1), r, w)

    def stt(self, out, a, s, b, op0, op1):
        r, w = _rw([out], [a, s, b])
        self.P.op("dve", lambda e: e.scalar_tensor_tensor(out=_a(out), in0=_a(a), scalar=_a(s), in1=_a(b), op0=op0, op1=op1), r, w)

    def cp(self, eng, out, in_):
        r, w = _rw([out], [in_])
        if eng == "act":
            self.P.op(eng, lambda e: e.copy(out=_a(out), in_=_a(in_)), r, w)
        else:
            self.P.op(eng, lambda e: e.tensor_copy(out=_a(out), in_=_a(in_)), r, w)

    def memset(self, eng, out, val):
        r, w = _rw([out], [])
        self.P.op(eng, lambda e: e.memset(_a(out), val), r, w)

    def mm(self, out, lhsT, rhs, start=True, stop=True):
        r, w = _rw([out], [lhsT, rhs])
        self.P.op("pe", lambda e: e.matmul(_a(out), lhsT=_a(lhsT), rhs=_a(rhs), start=start, stop=stop, skip_group_check=True), r, w)

    def tr(self, out, in_, ident):
        r, w = _rw([out], [in_, ident])
        self.P.op("pe", lambda e: e.transpose(out=_a(out), in_=_a(in_), identity=_a(ident)), r, w)

    def scan(self, out, d0, d1, init, op0, op1):
        r, w = _rw([out], [d0, d1, init])
        self.P.op("dve", lambda e: e.tensor_tensor_scan(out=_a(out), data0=_a(d0), data1=_a(d1), initial=_a(init), op0=op0, op1=op1), r, w)

    def dma(self, eng, sem, pairs, extra_r=(), extra_w=()):
        r, w = [], []
        fns = []
        for (o, i) in pairs:
            if o.key is not None:
                w.append(o.key)
            if i.key is not None:
                r.append(i.key)
            fns.append((lambda oo, ii: (lambda e: e.dma_start(out=oo, in_=ii)))(_a(o), _a(i)))
        self.P.dma(eng, fns, sem, list(r) + list(extra_r), list(w) + list(extra_w))


class Arena:
    def __init__(self, big_ap, nbytes):
        self.big = big_ap; self.n = nbytes; self.off = 0; self.uid = 0

    def alloc(self, name, free_shape, dt, parts=128):
        esz = 4 if dt == F32 else 2
        n = 1
        for s in free_shape:
            n *= s
        nb = (n * esz + 31) // 32 * 32
        assert self.off + nb <= self.n, f"SBUF arena overflow at {name}: {self.off}+{nb} > {self.n}"
        a = self.big[:, self.off // 4:(self.off + nb) // 4]
        self.off += nb
        if dt != F32:
            a = a.bitcast(dt)
        a = a[:, 0:n]
        if len(free_shape) == 2:
            a = a.rearrange("p (a b) -> p a b", a=free_shape[0])
        elif len(free_shape) == 3:
            a = a.rearrange("p (a b c) -> p a b c", a=free_shape[0], b=free_shape[1])
        if parts != 128:
            a = a[0:parts]
        self.uid += 1
        return T_(a, f"{name}#{self.uid}")

    def mark(self):
        return self.off

    def release(self, m):
        self.off = m

D = 1024
NIN = 10016
DFF = 2816
G = 512
GT = G // 128
EPS = 1e-6
DSC = 0.6065306597
NCOLP = 132
NROWP = 4688
ARENA = 190 * 1024

WIN_BLOCKS = [("lr", 2592 + 1536, 256), ("rr", 2592, 512), ("rk", 2592 + 512, 512), ("rv", 2592 + 1024, 512),
              ("sx0", 1024, 512), ("sx1", 1536, 512), ("sbc", 2048, 512),
              ("hq", 4384, 512), ("hff", 4896, 512), ("hfb", 5408, 512),
              ("z0", 0, 512), ("z1", 512, 512), ("dt", 2560, 32), ("hi", 5920, 512), ("hg", 6432, 512)] + \
             [(f"g{j}", 6944 + 512 * j, 512) for j in range(6)]
WIN_IDX = {n: i for i, (n, _, _) in enumerate(WIN_BLOCKS)}
FIN_BLOCKS = [(f"fg{b}", 512 * b, min(512, 2816 - 512 * b)) for b in range(6)] + \
             [(f"fu{b}", 2816 + 512 * b, min(512, 2816 - 512 * b)) for b in range(6)]


def make_consts():
    p = np.arange(128)[:, None]
    q = np.arange(128)[None, :]
    c = {}
    c["ident"] = (p == q)
    c["tri_f"] = (p <= q)
    c["tri_b"] = (p >= q)
    c["str_f"] = (p > q)
    c["str_b"] = (p < q)
    c["ones"] = np.ones((128, 128))
    c["negm_f"] = np.where(q >= p, 0.0, -30000.0)
    c["negm_b"] = np.where(q <= p, 0.0, -30000.0)
    ls = p % 64
    lt = q % 64
    same = (p // 64) == (q // 64)
    for d, before_strict, before_incl in (("f", ls < lt, ls <= lt), ("b", ls > lt, ls >= lt)):
        mu = np.where(q < 64, -1.0 * before_strict, 1.0 * before_incl)
        mk = np.where(q < 64, 1.0 * before_strict, 1.0 * before_incl)
        c["mu_" + d] = mu
        c["mk_" + d] = mk
        ss = np.arange(64)[None, :]
        lt2 = (np.arange(128) % 64)[:, None]
        c["mx_" + d] = np.concatenate([-1.0 * ((ss < lt2) if d == "f" else (ss > lt2)), np.zeros((128, 64))], 1)
        c["mh_" + d] = 1.0 * (same & before_incl)
    c["bones"] = 1.0 * same
    bs = np.zeros((128, 128)); bs[:64, 0] = 1; bs[64:, 1] = 1
    c["bsel"] = bs
    names = list(c.keys())
    arr = np.concatenate([np.asarray(c[n], np.float32) for n in names], 1)
    sel = np.zeros((128, 16, 128), np.float32)
    for h in range(16):
        sel[h, h, :] = 1.0
    arr = np.concatenate([arr, sel.reshape(128, -1)], 1)
    off = {n: i * 128 for i, n in enumerate(names)}
    off["sel"] = len(names) * 128
    return np.ascontiguousarray(arr), off


CONSTS, COFF = make_consts()
NCONST = CONSTS.shape[1]


def build(T, depth, dbg=(), stop=None, skip=(), interleave=True, npairs=0):
    NT = T // 128
    NG = T // G
    nc = bass.Bass("TRN2", target_bir_lowering=False)
    dbg = set(dbg)

    def din(name, shape, dt=F32):
        return nc.dram_tensor(name, list(shape), dt, kind="ExternalInput").ap()

    def dscr(name, shape, dt):
        kind = "ExternalOutput" if name in dbg else "Internal"
        return nc.dram_tensor(name, list(shape), dt, kind=kind).ap()

    x_in = din("x", [T, D])
    w_in = din("w_in", [depth, D, NIN])
    w_bs = din("w_branch_ssm", [depth, 1024, D])
    w_br = din("w_branch_rwkv", [depth, 512, D])
    w_bh = din("w_branch_hgrn", [depth, 512, D])
    w_o = din("w_out", [depth, D, D])
    w_fi = din("ffn_w_in", [depth, D, 2 * DFF])
    w_fd = din("ffn_w_down", [depth, DFF, D])
    w_up = din("rwkv_w_up", [depth, 2, 64, 512])
    a_up = din("rwkv_a_up", [depth, 64, 512])
    g_up = din("rwkv_g_up", [depth, 128, 512])
    colp = din("colp", [depth, 128, NCOLP])
    rowp = din("rowp", [depth, 1, NROWP])
    fnw = din("fnw", [1, D])
    consts_d = din("consts", [128, NCONST])
    out_d = nc.dram_tensor("out", [T, D], F32, kind="ExternalOutput").ap()
    paired = npairs > 0
    if paired:
        xhalo = din("xhalo", [2, D])
        groups = [[2 * i, 2 * i + 1] for i in range(npairs)]
        EXs = nc.dram_tensor("EXs", [128, 1792], F32).ap()
        EXr = nc.dram_tensor("EXr", [128, 1792], F32).ap()
        HX = nc.dram_tensor("HX", [128, 16], F32).ap()
        HXr = nc.dram_tensor("HXr", [128, 16], F32).ap()
        HD = nc.dram_tensor("HD", [2, D], F32).ap()

    WIN = dscr("WIN", [depth, len(WIN_BLOCKS), 128, 8, 512], BF16)
    WBS = dscr("WBS", [depth, 2, 128, 8, 512], BF16)
    WBR = dscr("WBR", [depth, 2, 128, 4, 512], BF16)
    WBH = dscr("WBH", [depth, 2, 128, 4, 512], BF16)
    WO = dscr("WO", [depth, 2, 128, 8, 512], BF16)
    FIN = dscr("FIN", [depth, 12, 128, 8, 512], BF16)
    FDN = dscr("FDN", [depth, 2, 128, 22, 512], BF16)
    XS = dscr("XS", [T, 1024], BF16)
    BTM = dscr("BTM", [T, 256], BF16)
    BTs = dscr("BTs", [2, 128, T], BF16)
    CTs = dscr("CTs", [2, 128, T], BF16)
    DTs = dscr("DTs", [T, 32], F32)
    ZS = dscr("ZS", [T, 1024], BF16)
    RW = dscr("RW", [16, 128, T], BF16)
    SW = dscr("SW", [8, 128, T], F32)
    VR = dscr("VR", [T, 512], BF16)
    GR = dscr("GR", [T, 512], BF16)
    BC = dscr("BC", [T, 8], F32)
    QH = dscr("QH", [4, 128, T], BF16)
    KH = dscr("KH", [8, 128, T], BF16)
    LF = dscr("LF", [8, 128, T], F32)
    IH = dscr("IH", [T, 512], BF16)
    GH = dscr("GH", [T, 512], BF16)
    GTs = dscr("GTs", [T, 3072], BF16)
    YS = dscr("YS", [2, T, 1024], F32)
    YR = dscr("YR", [2, T, 512], F32)
    YH = dscr("YH", [2, T, 512], F32)
    XM = dscr("XM", [T, D], F32)
    X1 = dscr("X1", [T, D], F32)

    def DR(ap, key=None):
        return T_(ap, key)

    es = contextlib.ExitStack()
    big = es.enter_context(nc.sbuf_tensor("big", [128, ARENA // 4], F32))
    A = Arena(big, ARENA)
    PS = [T_(es.enter_context(nc.psum_tensor(f"ps{i}", [128, 512], F32))[:], f"ps{i}", True) for i in range(8)]
    P = Prog(nc)
    O = Ops(P)
    cnt = [0]

    def rr(engs):
        cnt[0] += 1
        return engs[cnt[0] % len(engs)]

    cst = A.alloc("cst", [NCONST], F32)
    O.dma("sp", "ld_c", [(cst, DR(consts_d[:, :]))])

    def cf(name, w=128):
        o = COFF[name]
        return cst[:, o:o + w]

    identb = A.alloc("identb", [128], BF16)
    negmb = {d: A.alloc("negm" + d, [128], BF16) for d in "fb"}
    bonesb = A.alloc("bonesb", [128], BF16)
    bselb = A.alloc("bselb", [2], BF16)
    onesw = A.alloc("onesw", [512], F32)
    O.cp("dve", identb, cf("ident"))
    for d in "fb":
        O.cp("dve", negmb[d], cf("negm_" + d))
    O.cp("dve", bonesb, cf("bones"))
    O.cp("dve", bselb, cf("bsel", 2))
    O.memset("pool", onesw, 1.0)
    SEL = cst[0:16, COFF["sel"]:COFF["sel"] + 2048].v("p (h s) -> p h s", h=16)
    fnwb = A.alloc("fnwb", [D], F32)
    O.dma("sp", "ld_f", [(fnwb, DR(fnw[0:1, :].to_broadcast([128, D])))])

    cp_ = A.alloc("colp", [NCOLP], F32)
    rp = A.alloc("rowp", [NROWP], F32)
    der = A.alloc("der", [64], F32)
    arow = A.alloc("arow", [32], F32)
    lrw = A.alloc("lrw", [1024], BF16)
    gupb = A.alloc("gupb", [512], BF16)
    c0 = der[:, 0:14]; omka = der[:, 14:18]; lb = der[:, 18:22]; oml = der[:, 22:26]
    cw = lambda c, k: cp_[:, c * 5 + k:c * 5 + k + 1]
    cb = lambda c: cp_[:, 60 + c:61 + c]
    mu = lambda c, j: cp_[:, 72 + c * 2 + j:73 + c * 2 + j]
    w0c = lambda d, c: cp_[:, 100 + d * 4 + c:101 + d * 4 + c]
    a0c = lambda c: cp_[:, 108 + c:109 + c]
    kkc = lambda c: cp_[:, 112 + c:113 + c]
    kac = lambda c: cp_[:, 116 + c:117 + c]
    rkc = lambda c: cp_[:, 120 + c:121 + c]
    n1w = rp[:, 0:1024]; n2w = rp[:, 1024:2048]; snw = rp[:, 2048:3072]
    lnw = rp[:, 3072:3584]; lnb = rp[:, 3584:4096]; hnw = rp[:, 4096:4608]
    dtb = rp[:, 4608:4640]; alog = rp[:, 4640:4672]; dskip = rp[:, 4672:4688]

    pers_mark = A.mark()

    def prologue():
        m = A.mark()
        stf = [A.alloc(f"stf{i}", [8, 512], F32) for i in range(2)]
        stb = [A.alloc(f"stb{i}", [8, 512], BF16) for i in range(2)]
        u = [0]

        def unit(src, K0, kc, c0_, ncols, dst):
            i = u[0] % 2
            u[0] += 1
            s_ap = src[K0:K0 + kc * 128, c0_:c0_ + ncols].rearrange("(k p) n -> p k n", p=128)
            O.dma("sp", f"pl{i}", [(stf[i][:, 0:kc, 0:ncols], DR(s_ap))])
            O.cp(rr(["dve", "act"]), stb[i][:, 0:kc, 0:ncols], stf[i][:, 0:kc, 0:ncols])
            O.dma("pool", f"ps{i}", [(DR(dst[:, 0:kc, 0:ncols]), stb[i][:, 0:kc, 0:ncols])])

        for l in range(depth):
            for bi, (nm, c0_, ncols) in enumerate(WIN_BLOCKS):
                unit(w_in[l], 0, 8, c0_, ncols, WIN[l, bi])
            for cbk in range(2):
                unit(w_bs[l], 0, 8, cbk * 512, 512, WBS[l, cbk])
                unit(w_br[l], 0, 4, cbk * 512, 512, WBR[l, cbk])
                unit(w_bh[l], 0, 4, cbk * 512, 512, WBH[l, cbk])
                unit(w_o[l], 0, 8, cbk * 512, 512, WO[l, cbk])
                for k0 in (0, 8, 16):
                    kc = min(8, 22 - k0)
                    unit(w_fd[l], k0 * 128, kc, cbk * 512, 512, FDN[l, cbk][:, k0:k0 + kc, :])
            for bi, (nm, c0_, ncols) in enumerate(FIN_BLOCKS):
                unit(w_fi[l], 0, 8, c0_, ncols, FIN[l, bi])
        P.barrier()
        A.release(m)

    def layer_setup(l):
        m = A.mark()
        O.dma("sp", "ld_p", [(cp_, DR(colp[l])), (rp, DR(rowp[l, 0:1, :].to_broadcast([128, NROWP])))])
        stg = A.alloc("lrstg", [1024], F32)
        O.dma("sp", "ld_c2", [(stg[0:64, 0:512], DR(w_up[l, 0])), (stg[0:64, 512:1024], DR(w_up[l, 1])),
                              (stg[64:128, 0:512], DR(a_up[l]))])
        O.cp("dve", lrw[0:64, :], stg[0:64, :])
        O.cp("dve", lrw[64:128, 0:512], stg[64:128, 0:512])
        stg2 = A.alloc("gstg", [512], F32)
        O.dma("sp", "ld_c3", [(stg2, DR(g_up[l]))])
        O.cp("dve", gupb, stg2)
        muv = cp_[:, 72:100].v("p (c j) -> p c j", j=2)
        O.tt("dve", c0, muv[:, :, 0], muv[:, :, 1], ALU.add)
        O.ts("dve", c0, c0, -1.0, ALU.mult, 1.0, ALU.add)
        O.ts("dve", omka, cp_[:, 116:120], -1.0, ALU.mult, 1.0, ALU.add)
        if l == 0:
            O.memset("dve", lb, 0.0)
        else:
            O.tt("dve", lb, cp_[:, 128:132], cp_[:, 124:128], ALU.subtract)
            O.act(lb, lb, AF.Sigmoid)
        O.ts("dve", oml, lb, -1.0, ALU.mult, 1.0, ALU.add)
        O.act(arow, alog, AF.Exp)
        O.ts("dve", arow, arow, -1.0, ALU.mult)
        P.barrier()
        A.release(m)

    def norm_T(xin, npart, wrow, xn, sq, ss, outT_cols, ps_bank):
        O.act(sq[0:npart], xin[0:npart], AF.Square, accum=ss[0:npart])
        O.act(ss[0:npart], ss[0:npart], AF.Ln, bias=EPS, scale=1.0 / D)
        O.act(ss[0:npart], ss[0:npart], AF.Exp, scale=-0.5)
        O.stt(xn[0:npart], xin[0:npart], ss[0:npart, 0:1], wrow[0:npart], ALU.mult, ALU.mult)
        psb = ps_bank.cast(BF16)
        for kc in range(8):
            O.tr(psb[:, kc * 128:kc * 128 + npart], xn[0:npart, kc * 128:(kc + 1) * 128], identb[0:npart, 0:npart])
        return psb

    def phaseA(l, xsrc, rhalo=None):
        m = A.mark()
        xin = [A.alloc(f"xin{i}", [D], F32) for i in range(2)]
        xh = A.alloc("xh", [D], F32)
        sq = A.alloc("sq", [D], BF16)
        ss = A.alloc("ss", [1], F32)
        xn = [A.alloc(f"xn{i}", [D], BF16) for i in range(2)]
        xnT = [A.alloc(f"xnT{i}", [8, G], BF16) for i in range(2)]
        xnTh = [A.alloc(f"xnTh{i}", [8, 4], BF16) for i in range(2)]
        wst = [A.alloc(f"wst{i}", [8, 512], BF16) for i in range(3)]
        rawc = [A.alloc(f"rawc{i}", [G + 4], BF16) for i in range(3)]
        dgc = A.alloc("dgc", [12, 5, 128], BF16)
        dgs = A.alloc("dgs", [14, 3, 128], BF16)
        for c in range(12):
            for k in range(5):
                O.ts("dve", dgc[:, c, k, :], cf("ident"), cw(c, k), ALU.mult)
        for c in range(14):
            O.ts("dve", dgs[:, c, 0, :], cf("ident"), c0[:, c:c + 1], ALU.mult)
            O.ts("dve", dgs[:, c, 1, :], cf("ident"), mu(c, 0), ALU.mult)
            O.ts("dve", dgs[:, c, 2, :], cf("ident"), mu(c, 1), ALU.mult)
        fmb = [A.alloc(f"fmb{i}", [G], BF16) for i in range(4)]
        fmf = [A.alloc(f"fmf{i}", [G], F32) for i in range(3)]
        xsT = A.alloc("xsT", [8, G], BF16)
        bT = A.alloc("bT", [2, G], BF16)
        tmb = [A.alloc(f"tmb{i}", [1024], BF16) for i in range(3)]
        tmf = [A.alloc(f"tmf{i}", [32], F32) for i in range(2)]
        aT = A.alloc("aT", [4, G], F32)
        rT = A.alloc("rT", [4, G], BF16)
        vT = A.alloc("vT", [4, G], BF16)
        twad = A.alloc("twad", [G], BF16)
        sg = A.alloc("sg", [G], BF16)
        kf = A.alloc("kf", [G], F32)
        t1 = A.alloc("t1", [G], F32)
        t2 = A.alloc("t2", [G], F32)
        sqb = A.alloc("sqb", [G], BF16)
        kmb = A.alloc("kmb", [G], BF16)
        prodT = A.alloc("prodT", [G], BF16)
        bcst = A.alloc("bcst", [GT, 8], F32)
        ctr = dict(w=0, raw=0, acc=0, fmb=0, fmf=0, tmb=0, tmf=0, fm=0, misc=0)

        def nxt(name, n):
            ctr[name] += 1
            return ctr[name] % n

        def misc_ps():
            return PS[4 + nxt("misc", 3)]

        for gi in range(NG):
            t0 = gi * G
            xt_ = xnT[gi % 2]
            xth = xnTh[gi % 2]
            for j in range(GT):
                xi = xin[j % 2]
                O.dma("sp", f"ldx{j % 2}", [(xi, DR(xsrc[t0 + j * 128:t0 + (j + 1) * 128, :]))])
                psb = norm_T(xi, 128, n1w, xn[j % 2], sq, ss, None, PS[3])
                O.cp(rr(["act", "dve"]), xt_[:, :, j * 128:(j + 1) * 128], psb.v("p (k t) -> p k t", k=8))
            O.memset("pool", xh[0:4], 0.0)
            prs = []
            if t0 > 0:
                prs.append((xh[0:2], DR(xsrc[t0 - 2:t0, :])))
            if t0 + G < T:
                prs.append((xh[2:4], DR(xsrc[t0 + G:t0 + G + 2, :])))
            elif rhalo is not None:
                prs.append((xh[2:4], DR(rhalo[0:2, :], "rhalo")))
            if prs:
                O.dma("sp", "ldxh", prs)
            psb = norm_T(xh, 4, n1w, xn[0], sq, ss, None, PS[3])
            O.cp("dve", xth, psb.v("p (k t) -> p k t", k=8)[:, :, 0:4])

            def load_w(bi):
                w = wst[nxt("w", 3)]
                ncw = WIN_BLOCKS[bi][2]
                O.dma("sp", f"ldw{ctr['w'] % 3}", [(w[:, :, 0:ncw], DR(WIN[l, bi][:, :, 0:ncw]))])
                return w

            def fm_chunk(w, c, halo):
                ps = PS[(0, 1, 3)[nxt("fm", 3)]]
                for kc in range(8):
                    O.mm(ps[:, 0:G], w[:, kc, c * 128:(c + 1) * 128], xt_[:, kc, :], start=(kc == 0), stop=(kc == 7))
                if not halo:
                    return ps, None
                hoff = (ctr["fm"] % 8) * 4
                ph = PS[2][:, hoff:hoff + 4]
                for kc in range(8):
                    O.mm(ph, w[:, kc, c * 128:(c + 1) * 128], xth[:, kc, :], start=(kc == 0), stop=(kc == 7))
                rc = rawc[nxt("raw", 3)]
                O.cp("act", rc[:, 2:G + 2], ps[:, 0:G])
                O.cp("dve", rc[:, 0:2], ph[:, 0:2])
                O.cp("dve", rc[:, G + 2:G + 4], ph[:, 2:4])
                return ps, rc

            def shift_mm(rc, c):
                pc = misc_ps()
                O.mm(pc[:, 0:G], dgs[:, c, 0, :], rc[:, 2:G + 2], start=True, stop=False)
                O.mm(pc[:, 0:G], dgs[:, c, 1, :], rc[:, 1:G + 1], start=False, stop=False)
                O.mm(pc[:, 0:G], dgs[:, c, 2, :], rc[:, 3:G + 3], start=False, stop=True)
                return pc

            def g_shifted(rc, c, out, after=None):
                pc = shift_mm(rc, c)
                yield
                O.cp(rr(["act", "dve"]), out, pc[:, 0:G])
                if after is not None:
                    after()

            def g_conv(rc, c, out, after=None):
                pc = misc_ps()
                for k in range(5):
                    O.mm(pc[:, 0:G], dgc[:, c, k, :], rc[:, k:k + G], start=(k == 0), stop=(k == 4))
                yield
                O.act(out, pc[:, 0:G], AF.Silu, bias=cb(c))
                if after is not None:
                    after()

            def pipeline(tasks, depth=2):
                live = []
                it = iter(tasks)
                done = False
                while True:
                    if not done and len(live) < depth:
                        try:
                            live.append(next(it)())
                        except StopIteration:
                            done = True
                    if not live:
                        if done:
                            break
                        continue
                    for g_ in list(live):
                        try:
                            next(g_)
                        except StopIteration:
                            live.remove(g_)

            def store_fm(dst3, ci, src, sem):
                O.dma("pool", sem, [(DR(dst3[ci][:, t0:t0 + G]), src)])

            w = load_w(WIN_IDX["lr"])
            ps, rc = fm_chunk(w, 0, True)
            pc = shift_mm(rc, 12)
            O.act(twad[0:64], pc[0:64, 0:G], AF.Tanh)
            O.cp("dve", twad[64:128], pc[64:128, 0:G])
            for d in range(2):
                for j in range(4):
                    pm = misc_ps()
                    O.mm(pm[:, 0:G], lrw[0:64, d * 512 + j * 128:d * 512 + (j + 1) * 128], twad[0:64, :])
                    f = fmf[nxt("fmf", 3)]
                    O.act(f, pm[:, 0:G], AF.Sigmoid, bias=w0c(d, j))
                    store_fm(SW, d * 4 + j, f, f"sfmf{ctr['fmf'] % 3}")
            for j in range(4):
                pm = misc_ps()
                O.mm(pm[:, 0:G], lrw[64:128, j * 128:(j + 1) * 128], twad[64:128, :])
                O.act(aT[:, j, :], pm[:, 0:G], AF.Sigmoid, bias=a0c(j))
            ps, rc = fm_chunk(w, 1, True)
            pc = shift_mm(rc, 13)
            O.act(sg, pc[:, 0:G], AF.Sigmoid)
            for j in range(GT):
                pm = misc_ps()
                O.mm(pm[:, 0:512], sg[:, j * 128:(j + 1) * 128], gupb)
                tb = tmb[nxt("tmb", 3)]
                O.cp(rr(["act", "dve"]), tb[:, 0:512], pm[:, 0:512])
                O.dma("pool", f"stmb{ctr['tmb'] % 3}", [(DR(GR[t0 + j * 128:t0 + (j + 1) * 128, :]), tb[:, 0:512])])
            w = load_w(WIN_IDX["rr"])
            def task_r(c, w=w):
                def gen():
                    ps, rc = fm_chunk(w, c, True)
                    yield
                    yield from g_shifted(rc, c, rT[:, c, :], lambda: store_fm(RW, 0 * 4 + c, rT[:, c, :], "srT"))
                return gen
            pipeline([task_r(c) for c in range(4)])
            w = load_w(WIN_IDX["rk"])
            for c in range(4):
                ps, rc = fm_chunk(w, c, True)
                pc = shift_mm(rc, 4 + c)
                O.cp("act", kf, pc[:, 0:G])
                O.act(sqb, kf, AF.Square, scale=kkc(c))
                pm = misc_ps()
                O.mm(pm[:, 0:G], bonesb, sqb)
                O.act(t1, pm[:, 0:G], AF.Ln, bias=1e-12)
                O.act(t1, t1, AF.Exp, scale=-0.5)
                kkb = fmb[nxt("fmb", 4)]
                O.stt(kkb, kf, kkc(c), t1, ALU.mult, ALU.mult)
                store_fm(RW, 2 * 4 + c, kkb, f"sfmb{ctr['fmb'] % 4}")
                ub = fmb[nxt("fmb", 4)]
                O.tt("dve", ub, kkb, aT[:, c, :], ALU.mult)
                store_fm(RW, 3 * 4 + c, ub, f"sfmb{ctr['fmb'] % 4}")
                O.ts("dve", t2, aT[:, c, :], kac(c), ALU.mult, omka[:, c:c + 1], ALU.add)
                kmb_ = fmb[nxt("fmb", 4)]
                O.tt("dve", kmb_, kf, t2, ALU.mult)
                store_fm(RW, 1 * 4 + c, kmb_, f"sfmb{ctr['fmb'] % 4}")
                O.stt(prodT, rT[:, c, :], rkc(c), kmb_, ALU.mult, ALU.mult)
                for j in range(GT):
                    O.mm(PS[7][:, j * 8 + 2 * c:j * 8 + 2 * c + 2], prodT[:, j * 128:(j + 1) * 128], bselb)
            O.cp("dve", bcst, PS[7][:, 0:GT * 8].v("p (j e) -> p j e", j=GT))
            O.dma("pool", "sbc", [(DR(BC[t0:t0 + G, :].rearrange("(j p) e -> p j e", p=128)), bcst)])
            w = load_w(WIN_IDX["rv"])
            def task_v(c, w=w):
                def gen():
                    ps, rc = fm_chunk(w, c, True)
                    yield
                    yield from g_shifted(rc, 8 + c, vT[:, c, :])
                return gen
            pipeline([task_v(c) for c in range(4)])
            for j in range(GT):
                pm = misc_ps().cast(BF16)
                for c in range(4):
                    O.tr(pm[:, c * 128:(c + 1) * 128], vT[:, c, j * 128:(j + 1) * 128], identb)
                tb = tmb[nxt("tmb", 3)]
                O.cp(rr(["act", "dve"]), tb[:, 0:512], pm[:, 0:512])
                O.dma("pool", f"stmb{ctr['tmb'] % 3}", [(DR(VR[t0 + j * 128:t0 + (j + 1) * 128, :]), tb[:, 0:512])])
            xtasks = []
            for bi, nm in enumerate(("sx0", "sx1")):
                def task_x(bi, nm, c):
                    def gen():
                        if c == 0:
                            wx[bi] = load_w(WIN_IDX[nm])
                        ps, rc = fm_chunk(wx[bi], c, True)
                        yield
                        yield from g_conv(rc, bi * 4 + c, xsT[:, bi * 4 + c, :])
                    return gen
                xtasks += [task_x(bi, nm, c) for c in range(4)]
            wx = {}
            pipeline(xtasks)
            for j in range(GT):
                pm = misc_ps().cast(BF16)
                for c in range(8):
                    O.tr(pm[:, c * 128:(c + 1) * 128], xsT[:, c, j * 128:(j + 1) * 128], identb)
                tb = tmb[nxt("tmb", 3)]
                O.cp(rr(["act", "dve"]), tb, pm)
                O.dma("pool", f"stmb{ctr['tmb'] % 3}", [(DR(XS[t0 + j * 128:t0 + (j + 1) * 128, :]), tb)])
            w = load_w(WIN_IDX["sbc"])
            def task_bc(c, w=w):
                def gen():
                    ps, rc = fm_chunk(w, c, True)
                    yield
                    if c < 2:
                        yield from g_conv(rc, 8 + c, bT[:, c, :], lambda: store_fm(BTs, c, bT[:, c, :], "sbT"))
                    else:
                        f = fmb[nxt("fmb", 4)]
                        sem = f"sfmb{ctr['fmb'] % 4}"
                        yield from g_conv(rc, 8 + c, f, lambda: store_fm(CTs, c - 2, f, sem))
                return gen
            pipeline([task_bc(c) for c in range(4)])
            for j in range(GT):
                pm = misc_ps().cast(BF16)
                for c in range(2):
                    O.tr(pm[:, c * 128:(c + 1) * 128], bT[:, c, j * 128:(j + 1) * 128], identb)
                tb = tmb[nxt("tmb", 3)]
                O.cp(rr(["act", "dve"]), tb[:, 0:256], pm[:, 0:256])
                O.dma("pool", f"stmb{ctr['tmb'] % 3}", [(DR(BTM[t0 + j * 128:t0 + (j + 1) * 128, :]), tb[:, 0:256])])
            w = load_w(WIN_IDX["hq"])
            for c in range(4):
                ps, _ = fm_chunk(w, c, False)
                f = fmb[nxt("fmb", 4)]
                O.cp(rr(["act", "dve"]), f, ps[:, 0:G])
                store_fm(QH, c, f, f"sfmb{ctr['fmb'] % 4}")
            for d, nm in enumerate(("hff", "hfb")):
                w = load_w(WIN_IDX[nm])
                for c in range(4):
                    ps, _ = fm_chunk(w, c, False)
                    O.act(t1, ps[:, 0:G], AF.Sigmoid)
                    O.ts("dve", t2, t1, oml[:, c:c + 1], ALU.mult, lb[:, c:c + 1], ALU.add)
                    f = fmf[nxt("fmf", 3)]
                    O.act(f, t2, AF.Ln)
                    store_fm(LF, d * 4 + c, f, f"sfmf{ctr['fmf'] % 3}")
                    kb = fmb[nxt("fmb", 4)]
                    O.ts("dve", kb, t2, -1.0, ALU.mult, 1.0, ALU.add)
                    store_fm(KH, d * 4 + c, kb, f"sfmb{ctr['fmb'] % 4}")
            def tm_block(nm, ncols, post):
                w = load_w(WIN_IDX[nm])
                for j in range(GT):
                    pm = misc_ps()
                    for kc in range(8):
                        O.mm(pm[:, 0:ncols], xt_[:, kc, j * 128:(j + 1) * 128], w[:, kc, 0:ncols], start=(kc == 0), stop=(kc == 7))
                    post(j, pm)

            def post_store(func, dst, col0):
                def f(j, pm):
                    tb = tmb[nxt("tmb", 3)]
                    if func is None:
                        O.cp(rr(["act", "dve"]), tb[:, 0:512], pm[:, 0:512])
                    else:
                        O.act(tb[:, 0:512], pm[:, 0:512], func)
                    O.dma("pool", f"stmb{ctr['tmb'] % 3}", [(DR(dst[t0 + j * 128:t0 + (j + 1) * 128, col0:col0 + 512]), tb[:, 0:512])])
                return f

            tm_block("z0", 512, post_store(AF.Silu, ZS, 0))
            tm_block("z1", 512, post_store(AF.Silu, ZS, 512))

            def post_dt(j, pm):
                tf = tmf[nxt("tmf", 2)]
                O.tt("dve", tf, pm[:, 0:32], dtb, ALU.add)
                O.act(tf, tf, AF.Exp)
                O.act(tf, tf, AF.Ln, bias=1.0)
                O.dma("pool", f"stmf{ctr['tmf'] % 2}", [(DR(DTs[t0 + j * 128:t0 + (j + 1) * 128, :]), tf)])
            tm_block("dt", 32, post_dt)
            tm_block("hi", 512, post_store(None, IH, 0))
            tm_block("hg", 512, post_store(AF.Silu, GH, 0))
            for b in range(6):
                tm_block(f"g{b}", 512, post_store(AF.Sigmoid, GTs, 512 * b))
        P.barrier()
        A.release(m)

    id2 = A.alloc("id2", [64], F32)
    O.tt("dve", id2, cf("ident")[:, 0:64], cf("ident")[:, 64:128], ALU.add)
    pers_mark2 = A.mark()

    def scan_phase(l, d):
        m = A.mark()
        di = 0 if d == "f" else 1
        mloc = 31 if d == "f" else 32
        order = (0, 1) if d == "f" else (1, 0)
        tiles = list(range(NT)) if d == "f" else list(range(NT - 1, -1, -1))
        L = []
        for s in range(2):
            L.append(dict(
                xs=A.alloc(f"lxs{s}", [1024], BF16), btm=A.alloc(f"lbtm{s}", [256], BF16),
                bT=A.alloc(f"lbT{s}", [2, 128], BF16), cT=A.alloc(f"lcT{s}", [2, 128], BF16),
                dt=A.alloc(f"ldt{s}", [32], F32),
                rw=A.alloc(f"lrw{s}", [4, 4, 128], BF16), sw=A.alloc(f"lsw{s}", [4, 128], F32),
                vr=A.alloc(f"lvr{s}", [512], BF16),
                qh=A.alloc(f"lqh{s}", [4, 128], BF16), kh=A.alloc(f"lkh{s}", [4, 128], BF16),
                lf=A.alloc(f"llf{s}", [4, 128], F32), ih=A.alloc(f"lih{s}", [512], BF16)))

        def load(ti, s):
            b = L[s]
            t0 = ti * 128
            tsl = slice(t0, t0 + 128)
            prs = [(b["xs"], DR(XS[tsl, :])), (b["btm"], DR(BTM[tsl, :])),
                   (b["bT"], DR(BTs[:, :, tsl].rearrange("g n t -> n g t"))),
                   (b["cT"], DR(CTs[:, :, tsl].rearrange("g n t -> n g t"))),
                   (b["dt"], DR(DTs[tsl, :]))]
            O.dma("sp", f"lsA{s}", prs)
            prs = [(b["rw"][:, q, :, :], DR(RW[q * 4:(q + 1) * 4, :, tsl].rearrange("c p t -> p c t"))) for q in range(4)]
            prs += [(b["sw"], DR(SW[di * 4:(di + 1) * 4, :, tsl].rearrange("c p t -> p c t"))), (b["vr"], DR(VR[tsl, :]))]
            O.dma("sp", f"lsB{s}", prs)
            prs = [(b["qh"], DR(QH[:, :, tsl].rearrange("c p t -> p c t"))),
                   (b["kh"], DR(KH[di * 4:(di + 1) * 4, :, tsl].rearrange("c p t -> p c t"))),
                   (b["lf"], DR(LF[di * 4:(di + 1) * 4, :, tsl].rearrange("c p t -> p c t"))),
                   (b["ih"], DR(IH[tsl, :]))]
            O.dma("sp", f"lsC{s}", prs)

        dtA = A.alloc("dtA", [16], F32); nacum = A.alloc("nacum", [16], F32)
        Eall = A.alloc("Eall", [48], F32); w1 = A.alloc("w1", [16], F32)
        acT = A.alloc("acT", [128], F32)
        cbT = A.alloc("cbT", [2, 128], BF16)
        xdt = A.alloc("xdt", [1024], BF16); xdtd = A.alloc("xdtd", [1024], BF16)
        LT = [A.alloc(f"LT{i}", [4, 128], BF16) for i in range(2)]
        GTb = [A.alloc(f"GTb{i}", [4, 128], BF16) for i in range(2)]
        nacT = A.alloc("nacT", [128], F32)
        tmpf = A.alloc("tmpf", [512], F32)
        yss = [A.alloc(f"yss{i}", [1024], F32) for i in range(2)]
        Sf = A.alloc("Sf", [2, 512], F32); Sbf = A.alloc("Sbf", [2, 512], BF16)
        if not (paired and d == "b"):
            O.memset("dve", Sf, 0.0); O.memset("pool", Sbf, 0.0)

        def ssd_tile(ti, s):
            b = L[s]
            xs_, btm_, bT_, cT_, dt_ = b["xs"], b["btm"], b["bT"], b["cT"], b["dt"]
            dtd = dt_[:, di * 16:(di + 1) * 16]
            O.tt("dve", dtA, dtd, arow[:, di * 16:(di + 1) * 16], ALU.mult)
            psA = PS[0]
            O.mm(psA[:, 0:16], cf("tri_" + d), dtA)
            O.mm(psA[:, 16:32], cf("str_" + d), dtA)
            O.mm(psA[:, 32:48], cf("ones"), dtA)
            O.mm(psA[0:16, 64:192], dtA, cf("tri_" + d))
            O.ts("dve", nacum, psA[:, 0:16], -1.0, ALU.mult)
            O.act(Eall, psA[:, 0:48], AF.Exp)
            O.cp("act", acT[0:16], psA[0:16, 64:192])
            O.ts("dve", nacT[0:16], psA[0:16, 64:192], -1.0, ALU.mult)
            yield
            psB = PS[0]
            for g in range(2):
                O.mm(psB[:, g * 128:(g + 1) * 128], bT_[:, g, :], cT_[:, g, :])
            O.cp("act", cbT, psB[:, 0:256].v("p (g q) -> p g q", g=2))
            O.tt("dve", w1, dtd, Eall[:, 16:32], ALU.mult)
            xs3 = xs_.v("p (h e) -> p h e", h=16)
            O.tt("dve", xdt.v("p (h e) -> p h e", h=16), xs3, dtd.us(2).bc([128, 16, 64]), ALU.mult)
            O.tt("pool", xdtd.v("p (h e) -> p h e", h=16), xs3, w1.us(2).bc([128, 16, 64]), ALU.mult)
            yield
            ys_ = yss[ti % 2]
            for hq in range(4):
                g = hq // 2
                pseg = PS[1 + hq % 2]
                for i in range(4):
                    h = hq * 4 + i
                    reg = pseg[:, i * 128:(i + 1) * 128]
                    O.mm(reg, SEL[:, h, :], acT[0:16], start=True, stop=False)
                    O.mm(reg, nacT[0:16], SEL[:, h, :], start=False, stop=False)
                    O.mm(reg, identb, negmb[d], start=False, stop=True)
                lt = LT[hq % 2]
                O.act(lt, pseg.v("p (i q) -> p i q", i=4), AF.Exp)
                gt = GTb[hq % 2]
                O.tt("dve", gt, lt, cbT[:, g:g + 1, :].bc([128, 4, 128]), ALU.mult)
                yield
                for i in range(4):
                    h = hq * 4 + i
                    O.mm(PS[3][:, (h % 8) * 64:(h % 8 + 1) * 64], gt[:, i, :], xdt[:, h * 64:(h + 1) * 64], start=True, stop=True)
                if hq % 2 == 1:
                    O.cp("act", ys_[:, g * 512:(g + 1) * 512], PS[3][:, 0:512])
                yield
            for g in range(2):
                O.mm(PS[0][:, 0:512], cT_[:, g, :], Sbf[:, g, :])
                yield
                O.tt("dve", tmpf.v("p (h e) -> p h e", h=8), PS[0][:, 0:512].v("p (h e) -> p h e", h=8),
                     Eall[:, g * 8:(g + 1) * 8].us(2).bc([128, 8, 64]), ALU.mult)
                O.tt("dve", ys_[:, g * 512:(g + 1) * 512], ys_[:, g * 512:(g + 1) * 512], tmpf, ALU.add)
            O.dma("pool", f"sys{ti % 2}", [(DR(YS[di, ti * 128:(ti + 1) * 128, :]), ys_)])
            for g in range(2):
                O.mm(PS[0][:, 0:512], btm_[:, g * 128:(g + 1) * 128], xdtd[:, g * 512:(g + 1) * 512])
                yield
                S3 = Sf[:, g, :].v("p (h e) -> p h e", h=8)
                O.tt("dve", S3, S3, Eall[:, 32 + g * 8:32 + (g + 1) * 8].us(2).bc([128, 8, 64]), ALU.mult)
                O.tt("dve", Sf[:, g, :], Sf[:, g, :], PS[0][:, 0:512], ALU.add)
                O.cp("act", Sbf[:, g, :], Sf[:, g, :])
            yield

        rcs = A.alloc("rcs", [4, 128], F32); rex = A.alloc("rex", [4, 128], F32)
        rRS = A.alloc("rRS", [4, 128], F32)
        rEa = rcs; rEb = rex; rtm = rex
        SC2 = [A.alloc(f"SC{i}", [4, 2, 256], BF16) for i in range(2)]
        rsc = A.alloc("rsc", [3, 8], F32); reE2 = [A.alloc(f"reE{i}", [3, 8], F32) for i in range(2)]
        utkt2 = [A.alloc(f"utkt{i}", [2, 512], BF16) for i in range(2)]
        AU2 = [A.alloc(f"AU{i}", [8, 128], BF16) for i in range(2)]; AK2 = [A.alloc(f"AK{i}", [8, 128], BF16) for i in range(2)]
        TtF2 = [A.alloc(f"TtF{i}", [8, 64], BF16) for i in range(2)]
        Zb = [A.alloc(f"Zb{i}", [8, 64], BF16) for i in range(2)]
        Ztb = [A.alloc(f"Ztb{i}", [8, 64], BF16) for i in range(2)]
        Zbd = [A.alloc(f"Zbd{i}", [8, 128], BF16) for i in range(2)]
        Ztbd = [A.alloc(f"Ztbd{i}", [8, 128], BF16) for i in range(2)]
        for i in range(2):
            O.memset("pool", Zbd[i], 0.0); O.memset("pool", Ztbd[i], 0.0)
        Ttb = [A.alloc(f"Ttb{i}", [8, 64], BF16) for i in range(2)]
        RHSn = A.alloc("RHSn", [512], BF16); Nsb = A.alloc("Nsb", [512], BF16)
        Hr = A.alloc("Hr", [4, 64], F32); Hrp = A.alloc("Hrp", [4, 64], BF16); tmpH = A.alloc("tmpH", [4, 64], F32)
        yrw = [A.alloc(f"yrw{i}", [512], F32) for i in range(2)]
        if not (paired and d == "b"):
            O.memset("dve", Hr, 0.0)
        Hpm = [A.alloc(f"Hpm{i}", [4, 64], BF16) for i in range(2)]
        O.memset("pool", Hpm[0], 0.0); O.memset("pool", Hpm[1], 0.0)

        def v4(t):
            return t.v("p c (tc i) -> p c tc i", tc=2)


        def scalars(cs_, ex_, sc_, eE_, scale):
            c4, e4 = v4(cs_), v4(ex_)
            o = lambda k: sc_[:, k, :].v("p (c tc o) -> p c tc o", c=4, tc=2, o=1)
            O.tt("dve", o(0), c4[:, :, :, 63:64], e4[:, :, :, 0:1], ALU.subtract)
            if d == "f":
                O.tt("dve", o(1), c4[:, :, :, mloc:mloc + 1], e4[:, :, :, 0:1], ALU.subtract)
                O.tt("dve", o(2), c4[:, :, :, 63:64], c4[:, :, :, mloc:mloc + 1], ALU.subtract)
            else:
                O.tt("dve", o(1), c4[:, :, :, 63:64], e4[:, :, :, mloc:mloc + 1], ALU.subtract)
                O.tt("dve", o(2), e4[:, :, :, mloc:mloc + 1], e4[:, :, :, 0:1], ALU.subtract)
            O.act(eE_, sc_, AF.Exp, scale=scale)

        def esc(eE_, k, tc):
            return eE_[:, k, :].v("p (c tc) -> p c tc", tc=2)[:, :, tc:tc + 1]

        def rwkv_pre(ti, s, par):
            b = L[s]
            rw_, sw_, vr_ = b["rw"], b["sw"], b["vr"]
            SC, reE, utkt, AU, AK = SC2[par], reE2[par], utkt2[par], AU2[par], AK2[par]
            SCq = lambda q: SC[:, :, :, q * 64:(q + 1) * 64]
            sgn = -1.0 if d == "f" else 1.0
            fl = lambda t: t.v("p c t -> p (c t)")
            O.scan(fl(rcs), onesw, fl(sw_), 0.0, ALU.mult, ALU.add)
            O.tt("dve", rex, rcs, sw_, ALU.subtract)
            base = rcs if d == "f" else rex
            b4 = v4(base)
            O.tt("dve", v4(rRS), b4, b4[:, :, :, mloc:mloc + 1].bc([128, 4, 2, 64]), ALU.subtract)
            scalars(rcs, rex, rsc, reE, -DSC)
            O.act(rEa, rRS, AF.Exp, scale=sgn * DSC)
            O.tt("dve", SCq(3), v4(rw_[:, 0, :, :]), v4(rEa), ALU.mult)
            O.stt(rtm, rRS, sgn, sw_, ALU.mult, ALU.add)
            O.act(rEb, rtm, AF.Exp, scale=DSC)
            O.tt("dve", SCq(2), v4(rw_[:, 2, :, :]), v4(rEb), ALU.mult)
            yield
            O.act(rEa, rRS, AF.Exp, scale=-sgn * DSC)
            O.tt("dve", SCq(0), v4(rw_[:, 3, :, :]), v4(rEa), ALU.mult)
            O.tt("dve", SCq(1), v4(rw_[:, 1, :, :]), v4(rEa), ALU.mult)
            yield
            pT = PS[4].cast(BF16)
            for q in range(2):
                for c in range(4):
                    for tc in range(2):
                        O.tr(pT[tc * 64:(tc + 1) * 64, q * 512 + c * 128:q * 512 + (c + 1) * 128],
                             SC[:, c, tc, q * 64:(q + 1) * 64], identb)
            O.cp("act", utkt, pT.v("p (q x) -> p q x", q=2))
            yield
            bU = (PS[5], PS[6]); bX = (PS[7], PS[4])
            for c in range(4):
                for hh in range(2):
                    pr = slice(hh * 64, hh * 64 + 64)
                    for tc in range(2):
                        po = slice(tc * 64, tc * 64 + 64)
                        O.mm(bU[hh][po, c * 128:(c + 1) * 128], SC[pr, c, tc, 0:64], SC[pr, c, tc, 128:256], start=True, stop=True)
                        O.mm(bX[hh][po, c * 64:(c + 1) * 64], SC[pr, c, tc, 128:192], SC[pr, c, tc, 0:64], start=True, stop=True)
                if c % 2 == 1:
                    yield
            Zt_c = Ztb[0]
            for hh in range(2):
                O.tt("dve", AU[:, hh * 4:(hh + 1) * 4, :], bU[hh].v("p (h t) -> p h t", h=4),
                     cf("mu_" + d).us(1).bc([128, 4, 128]), ALU.mult)
                O.tt("dve", Zt_c[:, hh * 4:(hh + 1) * 4, :], bX[hh][:, 0:256].v("p (h t) -> p h t", h=4),
                     cf("mx_" + d, 64).us(1).bc([128, 4, 64]), ALU.mult)
            yield
            for c in range(4):
                for hh in range(2):
                    pr = slice(hh * 64, hh * 64 + 64)
                    for tc in range(2):
                        po = slice(tc * 64, tc * 64 + 64)
                        O.mm(bU[hh][po, c * 128:(c + 1) * 128], SC[pr, c, tc, 64:128], SC[pr, c, tc, 128:256], start=True, stop=True)
                if c % 2 == 1:
                    yield
            for hh in range(2):
                O.tt("dve", AK[:, hh * 4:(hh + 1) * 4, :], bU[hh].v("p (h t) -> p h t", h=4),
                     cf("mk_" + d).us(1).bc([128, 4, 128]), ALU.mult)
            Z_c = AU[:, :, 0:64]
            Tt_c = Ttb[0]
            O.tt("dve", Tt_c, Z_c, id2.us(1).bc([128, 8, 64]), ALU.add)
            yield
            def fill_bd(bd, src):
                O.cp("act", bd[0:64, :, 0:64], src[0:64])
                O.cp("dve", bd[64:128, :, 64:128], src[64:128])
            Zbd_c, Ztbd_c = Zbd[0], Ztbd[0]
            fill_bd(Zbd_c, Z_c)
            fill_bd(Ztbd_c, Zt_c)
            yield
            pZ, pZt, pTt = PS[4], PS[5], PS[7]
            for it in range(5):
                last = (it == 4)
                Zn, Ztn, Ttn = Zb[it % 2], Ztb[(it + 1) % 2], (TtF2[par] if last else Ttb[(it + 1) % 2])
                Zbd_n, Ztbd_n = Zbd[(it + 1) % 2], Ztbd[(it + 1) % 2]
                for j in range(8):
                    if not last:
                        O.mm(pZ[:, j * 64:(j + 1) * 64], Ztbd_c[:, j, :], Z_c[:, j, :], start=True, stop=True)
                    O.mm(pZt[:, j * 64:(j + 1) * 64], Zbd_c[:, j, :], Zt_c[:, j, :], start=True, stop=True)
                pZt3 = pZt.v("p (h t) -> p h t", h=8)
                O.cp("act", Ztbd_n[0:64, :, 0:64], pZt3[0:64])
                O.cp("dve", Ztbd_n[64:128, :, 64:128], pZt3[64:128])
                if not last:
                    pZ3 = pZ.v("p (h t) -> p h t", h=8)
                    O.cp("act", Ztn, pZt3)
                    O.cp("dve", Zn, pZ3)
                    O.cp("act", Zbd_n[0:64, :, 0:64], pZ3[0:64])
                    O.cp("dve", Zbd_n[64:128, :, 64:128], pZ3[64:128])
                yield
                for j in range(8):
                    O.mm(pTt[:, j * 64:(j + 1) * 64], Ztbd_n[:, j, :], Tt_c[:, j, :], start=True, stop=True)
                O.tt("dve", Ttn, pTt.v("p (h t) -> p h t", h=8), Tt_c, ALU.add)
                Z_c, Zt_c, Tt_c, Zbd_c, Ztbd_c = Zn, Ztn, Ttn, Zbd_n, Ztbd_n
                yield

        def rwkv_seq(ti, s, par):
            b = L[s]
            vr_ = b["vr"]
            SC, reE, utkt, AU, AK, Tt_c = SC2[par], reE2[par], utkt2[par], AU2[par], AK2[par], TtF2[par]
            yr_ = yrw[ti % 2]
            H3 = Hr
            for oi, tc in enumerate(order):
                po = slice(tc * 64, tc * 64 + 64)
                eAb = esc(reE, 1, tc).bc([128, 4, 64])
                O.tt("dve", Hpm[0][0:64], H3[0:64], eAb[0:64], ALU.mult)
                O.tt("dve", Hpm[1][64:128], H3[64:128], eAb[64:128], ALU.mult)
                bR = bN = PS[tc]
                for hd in range(8):
                    c, hh = hd // 2, hd % 2
                    j = hh * 4 + c
                    hs = slice(hd * 64, (hd + 1) * 64)
                    O.mm(bR[po, hs], SC[:, c, tc, 128:192], Hpm[hh][:, c, :], start=True, stop=False)
                    O.mm(bR[po, hs], AK[po, j, 0:64], vr_[po, hs], start=False, stop=True)
                O.ts("dve", RHSn[po], bR[po, :], -1.0, ALU.mult)
                yield
                for hd in range(8):
                    c, hh = hd // 2, hd % 2
                    j = hh * 4 + c
                    hs = slice(hd * 64, (hd + 1) * 64)
                    O.mm(bN[po, hs], Tt_c[po, j, :], RHSn[po, hs], start=True, stop=True)
                O.cp("act", Nsb[po], bN[po, :])
                yield
                for hd in range(8):
                    c, hh = hd // 2, hd % 2
                    j = hh * 4 + c
                    hs = slice(hd * 64, (hd + 1) * 64)
                    O.mm(PS[2][po, hs], SC[:, c, tc, 192:256], Hpm[hh][:, c, :], start=True, stop=False)
                    O.mm(PS[2][po, hs], AU[po, j, 64:128], Nsb[po, hs], start=False, stop=False)
                    O.mm(PS[2][po, hs], AK[po, j, 64:128], vr_[po, hs], start=False, stop=True)
                for hd in range(8):
                    c, hh = hd // 2, hd % 2
                    pr = slice(hh * 64, hh * 64 + 64)
                    hs = slice(hd * 64, (hd + 1) * 64)
                    O.mm(PS[3][pr, c * 64:(c + 1) * 64], utkt[po, 0, hs], Nsb[po, hs], start=True, stop=False)
                    O.mm(PS[3][pr, c * 64:(c + 1) * 64], utkt[po, 1, hs], vr_[po, hs], start=False, stop=True)
                O.tt("dve", tmpH, PS[3][:, 0:256].v("p (c v) -> p c v", c=4), esc(reE, 2, tc).bc([128, 4, 64]), ALU.mult)
                O.tt("dve", H3, H3, esc(reE, 0, tc).bc([128, 4, 64]), ALU.mult)
                O.tt("dve", H3, H3, tmpH, ALU.add)
                yield
            O.cp("act", yr_, PS[2][:, 0:512])
            O.dma("pool", f"syr{ti % 2}", [(DR(YR[di, ti * 128:(ti + 1) * 128, :]), yr_)])
            yield

        hcs = A.alloc("hcs", [4, 128], F32); hex_ = A.alloc("hex", [4, 128], F32)
        hRS = A.alloc("hRS", [4, 128], F32)
        hEa = hcs; hEb = hex_
        qt = A.alloc("qt", [4, 128], BF16); kt = A.alloc("kt", [4, 128], BF16)
        hsc = A.alloc("hsc", [3, 8], F32); heE = A.alloc("heE", [3, 8], F32)
        kttm = A.alloc("kttm", [512], BF16)
        Sm = A.alloc("Sm", [4, 128], BF16)
        Hs = A.alloc("Hs", [4, 128], F32); Hsp = A.alloc("Hsp", [4, 128], BF16); tmpS = A.alloc("tmpS", [4, 128], F32)
        yhw = [A.alloc(f"yhw{i}", [512], F32) for i in range(2)]
        if not (paired and d == "b"):
            O.memset("dve", Hs, 0.0)
        if paired and d == "b":
            exr = A.alloc("exr", [1792], F32); exs = A.alloc("exs", [1792], F32)
            O.dma("sp", "lex", [(exr, DR(EXr[:, :], "EXr")), (exs, DR(EXs[:, :], "EXs"))])
            O.tt("dve", Sf.v("p g x -> p (g x)"), exr[:, 0:1024], exs[:, 0:1024], ALU.subtract)
            O.cp("act", Sbf, Sf)
            O.tt("dve", Hr.v("p c v -> p (c v)"), exr[:, 1024:1280], exs[:, 1024:1280], ALU.subtract)
            O.tt("dve", Hs.v("p c v -> p (c v)"), exr[:, 1280:1792], exs[:, 1280:1792], ALU.subtract)

        def hgrn_tile(ti, s):
            b = L[s]
            qh_, kh_, lf_, ih_ = b["qh"], b["kh"], b["lf"], b["ih"]
            sgn = 1.0 if d == "f" else -1.0
            fl = lambda t: t.v("p c t -> p (c t)")
            O.scan(fl(hcs), onesw, fl(lf_), 0.0, ALU.mult, ALU.add)
            O.tt("dve", hex_, hcs, lf_, ALU.subtract)
            base = hcs if d == "f" else hex_
            b4 = v4(base)
            O.tt("dve", v4(hRS), b4, b4[:, :, :, mloc:mloc + 1].bc([128, 4, 2, 64]), ALU.subtract)
            scalars(hcs, hex_, hsc, heE, 1.0)
            O.act(hEa, hRS, AF.Exp, scale=sgn)
            O.tt("dve", qt, qh_, hEa, ALU.mult)
            O.act(hEb, hRS, AF.Exp, scale=-sgn)
            O.tt("dve", kt, kh_, hEb, ALU.mult)
            yield
            pT = PS[4].cast(BF16)
            for c in range(4):
                O.tr(pT[:, c * 128:(c + 1) * 128], kt[:, c, :], identb)
            O.cp("act", kttm, pT[:, 0:512])
            for c in range(4):
                O.mm(PS[5][:, c * 128:(c + 1) * 128], kt[:, c, :], qt[:, c, :], start=True, stop=True)
            O.tt("dve", Sm, PS[5].v("p (c t) -> p c t", c=4), cf("mh_" + d).us(1).bc([128, 4, 128]), ALU.mult)
            yield
            for tc in range(2):
                po = slice(tc * 64, tc * 64 + 64)
                for c in range(4):
                    O.mm(PS[6 + tc][:, c * 128:(c + 1) * 128], kttm[po, c * 128:(c + 1) * 128], ih_[po, c * 128:(c + 1) * 128],
                         start=True, stop=True)
            for c in range(4):
                O.mm(PS[4][:, c * 128:(c + 1) * 128], Sm[:, c, :], ih_[:, c * 128:(c + 1) * 128], start=(c == 0), stop=False)
            yield
            for oi, tc in enumerate(order):
                po = slice(tc * 64, tc * 64 + 64)
                O.tt("dve", Hsp, Hs, esc(heE, 1, tc).bc([128, 4, 128]), ALU.mult)
                for c in range(4):
                    O.mm(PS[4][po, c * 128:(c + 1) * 128], qt[:, c, tc * 64:(tc + 1) * 64], Hsp[:, c, :], start=False,
                         stop=(oi == 1))
                O.tt("dve", tmpS, PS[6 + tc].v("p (c v) -> p c v", c=4), esc(heE, 2, tc).bc([128, 4, 128]), ALU.mult)
                O.tt("dve", Hs, Hs, esc(heE, 0, tc).bc([128, 4, 128]), ALU.mult)
                O.tt("dve", Hs, Hs, tmpS, ALU.add)
                yield
            yh_ = yhw[ti % 2]
            O.cp("act", yh_, PS[4][:, 0:512])
            O.dma("pool", f"syh{ti % 2}", [(DR(YH[di, ti * 128:(ti + 1) * 128, :]), yh_)])
            yield

        branches = [b_ for b_ in ("ssd", "rwkv", "hgrn") if b_ not in skip]

        def run_streams(gens):
            if interleave:
                while gens:
                    for g_ in list(gens):
                        try:
                            next(g_)
                        except StopIteration:
                            gens.remove(g_)
            else:
                for g_ in gens:
                    for _ in g_:
                        pass

        load(tiles[0], 0)
        if "rwkv" in branches:
            run_streams([rwkv_pre(tiles[0], 0, 0)])
        for n, ti in enumerate(tiles):
            s = n % 2
            if n + 1 < len(tiles):
                load(tiles[n + 1], 1 - s)

            def streamA():
                if "ssd" in branches:
                    yield from ssd_tile(ti, s)
                if "rwkv" in branches:
                    yield from rwkv_seq(ti, s, n % 2)

            def streamB():
                if "hgrn" in branches:
                    yield from hgrn_tile(ti, s)
                if "rwkv" in branches and n + 1 < len(tiles):
                    yield from rwkv_pre(tiles[n + 1], 1 - s, (n + 1) % 2)
            run_streams([streamA(), streamB()])
        if paired and d == "f":
            O.dma("pool", "sex", [(DR(EXs[:, 0:1024], "EXs"), Sf.v("p g x -> p (g x)")),
                                  (DR(EXs[:, 1024:1280], "EXs"), Hr.v("p c v -> p (c v)")),
                                  (DR(EXs[:, 1280:1792], "EXs"), Hs.v("p c v -> p (c v)"))])
            P.barrier()
            P.cc("pool", lambda e: e.collective_compute("AllReduce", ALU.add, replica_groups=groups,
                                                         ins=[EXs.opt()], outs=[EXr.opt()]), f"ccs{l}", ["EXs"], ["EXr"])
        P.barrier()
        A.release(m)

    def head_rstd(ssv, n, scale, eps):
        O.act(ssv, ssv, AF.Ln, bias=eps, scale=scale)
        O.act(ssv, ssv, AF.Exp, scale=-0.5)

    def phaseE(l, xsrc):
        m = A.mark()
        wbs = A.alloc("wbs", [2, 8, 512], BF16); wbr = A.alloc("wbr", [2, 4, 512], BF16)
        wbh = A.alloc("wbh", [2, 4, 512], BF16); wo = A.alloc("wo", [2, 8, 512], BF16)
        O.dma("sp", "lwE", [(wbs[:, cbk], DR(WBS[l, cbk])) for cbk in range(2)] + [(wbr[:, cbk], DR(WBR[l, cbk])) for cbk in range(2)] +
              [(wbh[:, cbk], DR(WBH[l, cbk])) for cbk in range(2)] + [(wo[:, cbk], DR(WO[l, cbk])) for cbk in range(2)])
        xs2 = [A.alloc(f"exs{i}", [1024], BF16) for i in range(2)]; zs2 = [A.alloc(f"ezs{i}", [1024], BF16) for i in range(2)]
        ys2 = [A.alloc(f"eys{i}", [2, 1024], F32) for i in range(2)]
        yr_ = A.alloc("eyr", [2, 512], F32); yh_ = A.alloc("eyh", [2, 512], F32)
        vr_ = A.alloc("evr", [512], BF16); gr_ = A.alloc("egr", [512], BF16); gh_ = A.alloc("egh", [512], BF16)
        bc_ = A.alloc("ebc", [8], F32); gts_ = A.alloc("egts", [3072], BF16); xr_ = A.alloc("exr", [1024], F32)
        y = A.alloc("ey", [1024], F32); t = A.alloc("et", [1024], F32)
        sq = A.alloc("esq", [1024], BF16); ss = A.alloc("ess", [16], F32)
        ss_s = A.alloc("ess_s", [1], F32); ss_h = A.alloc("ess_h", [4], F32)
        ynb = A.alloc("eynb", [1024], BF16); ynT = A.alloc("eynT", [8, 128], BF16)
        ry = A.alloc("ery", [512], F32); rt = A.alloc("ert", [512], F32)
        rnb = A.alloc("ernb", [512], BF16); rnT = A.alloc("ernT", [4, 128], BF16)
        hy = A.alloc("ehy", [512], F32); ht = A.alloc("eht", [512], F32)
        hnb = A.alloc("ehnb", [512], BF16); hnT = A.alloc("ehnT", [4, 128], BF16)
        m1 = A.alloc("em1", [512], F32); m2 = A.alloc("em2", [512], F32)
        mixb = A.alloc("emixb", [1024], BF16); mT = A.alloc("emT", [8, 128], BF16)
        xo = [A.alloc(f"exo{i}", [1024], F32) for i in range(2)]
        def loadE1(ti):
            tsl = slice(ti * 128, (ti + 1) * 128)
            k = ti % 2
            O.dma("sp", f"lE1{k}", [(xs2[k], DR(XS[tsl, :])), (zs2[k], DR(ZS[tsl, :])), (ys2[k][:, 0, :], DR(YS[0, tsl, :])), (ys2[k][:, 1, :], DR(YS[1, tsl, :]))])

        def loadE2(ti):
            tsl = slice(ti * 128, (ti + 1) * 128)
            O.dma("sp", "lE2", [(yr_[:, 0, :], DR(YR[0, tsl, :])), (yr_[:, 1, :], DR(YR[1, tsl, :])), (vr_, DR(VR[tsl, :])), (gr_, DR(GR[tsl, :])), (bc_, DR(BC[tsl, :]))])
            O.dma("sp", "lE3", [(yh_[:, 0, :], DR(YH[0, tsl, :])), (yh_[:, 1, :], DR(YH[1, tsl, :])), (gh_, DR(GH[tsl, :]))])
            O.dma("sp", "lE4", [(gts_, DR(GTs[tsl, :])), (xr_, DR(xsrc[tsl, :]))])

        loadE1(0)
        loadE2(0)
        for ti in range(NT):
            tsl = slice(ti * 128, (ti + 1) * 128)
            xs_, zs_, ys_ = xs2[ti % 2], zs2[ti % 2], ys2[ti % 2]
            if ti + 1 < NT:
                loadE1(ti + 1)
            def g_ssd():
                O.tt("pool", y, ys_[:, 0, :], ys_[:, 1, :], ALU.add)
                yield
                O.tt("dve", t.v("p (h e) -> p h e", h=16), xs_.v("p (h e) -> p h e", h=16), dskip.us(2).bc([128, 16, 64]), ALU.mult)
                O.tt("dve", y, y, t, ALU.add)
                yield
                O.tt("dve", y, y, zs_, ALU.mult)
                O.act(sq, y, AF.Square, accum=ss_s[:, 0:1])
                yield
                head_rstd(ss_s[:, 0:1], 1, 1.0 / 1024, EPS)
                O.stt(ynb, y, ss_s[:, 0:1], snw, ALU.mult, ALU.mult)
                yield
                pT = PS[0].cast(BF16)
                for kc in range(8):
                    O.tr(pT[:, kc * 128:(kc + 1) * 128], ynb[:, kc * 128:(kc + 1) * 128], identb)
                O.cp("act", ynT, pT.v("p (k t) -> p k t", k=8))
                yield
                yield
            def g_rw():
                ry3 = ry.v("p (h e) -> p h e", h=8); rt3 = rt.v("p (h e) -> p h e", h=8)
                yield
                O.tt("pool", ry, yr_[:, 0, :], yr_[:, 1, :], ALU.add)
                P.op("dve", lambda e: e.tensor_reduce(out=ss[:, 0:8].ap, in_=ry3.ap, axis=AX.X, op=ALU.add), [ry.key], [ss.key])
                yield
                O.ts("dve", ss[:, 0:8], ss[:, 0:8], 1.0 / 64, ALU.mult)
                O.tt("dve", ry3, ry3, ss[:, 0:8].us(2).bc([128, 8, 64]), ALU.subtract)
                yield
                O.tt("dve", rt, ry, ry, ALU.mult)
                P.op("dve", lambda e: e.tensor_reduce(out=ss[:, 8:16].ap, in_=rt3.ap, axis=AX.X, op=ALU.add), [rt.key], [ss.key])
                yield
                head_rstd(ss[:, 8:16], 8, 1.0 / 64, 64e-5)
                O.tt("dve", ry3, ry3, ss[:, 8:16].us(2).bc([128, 8, 64]), ALU.mult)
                yield
                O.tt("pool", ry, ry, lnw, ALU.mult)
                O.tt("dve", ry, ry, lnb, ALU.add)
                yield
                O.tt("dve", rt3, vr_.v("p (h e) -> p h e", h=8), bc_.us(2).bc([128, 8, 64]), ALU.mult)
                O.tt("dve", ry, ry, rt, ALU.add)
                yield
                O.tt("dve", rnb, ry, gr_, ALU.mult)
                pT = PS[1].cast(BF16)
                yield
                for kc in range(4):
                    O.tr(pT[:, kc * 128:(kc + 1) * 128], rnb[:, kc * 128:(kc + 1) * 128], identb)
                    yield
                O.cp("act", rnT, pT[:, 0:512].v("p (k t) -> p k t", k=4))
                yield
            def g_hg():
                hy3 = hy.v("p (h e) -> p h e", h=4); ht3 = ht.v("p (h e) -> p h e", h=4)
                yield
                O.tt("pool", hy, yh_[:, 0, :], yh_[:, 1, :], ALU.add)
                O.tt("dve", ht, hy, hy, ALU.mult)
                yield
                P.op("dve", lambda e: e.tensor_reduce(out=ss_h[:, 0:4].ap, in_=ht3.ap, axis=AX.X, op=ALU.add), [ht.key], [ss_h.key])
                head_rstd(ss_h[:, 0:4], 4, 1.0 / 128, EPS)
                yield
                O.tt("dve", hy3, hy3, ss_h[:, 0:4].us(2).bc([128, 4, 128]), ALU.mult)
                O.tt("pool", hy, hy, hnw, ALU.mult)
                yield
                O.tt("dve", hnb, hy, gh_, ALU.mult)
                pT = PS[2].cast(BF16)
                yield
                for kc in range(4):
                    O.tr(pT[:, kc * 128:(kc + 1) * 128], hnb[:, kc * 128:(kc + 1) * 128], identb)
                    yield
                O.cp("act", hnT, pT[:, 0:512].v("p (k t) -> p k t", k=4))
                yield
            _g = [g_ssd(), g_rw(), g_hg()]
            while _g:
                for _x in list(_g):
                    try:
                        next(_x)
                    except StopIteration:
                        _g.remove(_x)
            for cbk in range(2):
                cs_ = slice(cbk * 512, (cbk + 1) * 512)
                for kc in range(8):
                    O.mm(PS[3][:, 0:512], ynT[:, kc, :], wbs[:, cbk, kc, :], start=(kc == 0), stop=(kc == 7))
                for kc in range(4):
                    O.mm(PS[4][:, 0:512], rnT[:, kc, :], wbr[:, cbk, kc, :], start=(kc == 0), stop=(kc == 3))
                for kc in range(4):
                    O.mm(PS[5][:, 0:512], hnT[:, kc, :], wbh[:, cbk, kc, :], start=(kc == 0), stop=(kc == 3))
                O.tt("dve", m1, PS[3][:, 0:512], gts_[:, cbk * 512:cbk * 512 + 512], ALU.mult)
                O.tt("dve", m2, PS[4][:, 0:512], gts_[:, 1024 + cbk * 512:1024 + cbk * 512 + 512], ALU.mult)
                O.tt("dve", m1, m1, m2, ALU.add)
                O.tt("dve", m2, PS[5][:, 0:512], gts_[:, 2048 + cbk * 512:2048 + cbk * 512 + 512], ALU.mult)
                O.tt("pool", mixb[:, cs_], m1, m2, ALU.add)
            pT = PS[6].cast(BF16)
            for kc in range(8):
                O.tr(pT[:, kc * 128:(kc + 1) * 128], mixb[:, kc * 128:(kc + 1) * 128], identb)
            O.cp("act", mT, pT.v("p (k t) -> p k t", k=8))
            xo_ = xo[ti % 2]
            for cbk in range(2):
                cs_ = slice(cbk * 512, (cbk + 1) * 512)
                for kc in range(8):
                    O.mm(PS[7][:, 0:512], mT[:, kc, :], wo[:, cbk, kc, :], start=(kc == 0), stop=(kc == 7))
                O.tt("dve", xo_[:, cs_], PS[7][:, 0:512], xr_[:, cs_], ALU.add)
            O.dma("pool", f"sxo{ti % 2}", [(DR(XM[tsl, :]), xo_)])
            if ti + 1 < NT:
                loadE2(ti + 1)
        P.barrier()
        A.release(m)

    def phaseD(l, last):
        m = A.mark()
        fdn = A.alloc("fdn", [2, 22, 512], BF16)
        O.dma("sp", "lwD", [(fdn[:, cbk], DR(FDN[l, cbk])) for cbk in range(2)])
        xin = [A.alloc(f"dxin{i}", [D], F32) for i in range(2)]
        sq = A.alloc("dsq", [D], BF16); ss = A.alloc("dss", [1], F32)
        xn = [A.alloc(f"dxn{i}", [D], BF16) for i in range(2)]
        xnT = A.alloc("dxnT", [8, G], BF16)
        wst = [A.alloc(f"dwst{i}", [8, 512], BF16) for i in range(3)]
        actT = A.alloc("actT", [22, G], BF16)
        sgt = [A.alloc(f"sgt{i}", [G], F32) for i in range(2)]
        xres = A.alloc("xres", [D], F32)
        xo = A.alloc("dxo", [D], F32)
        fo = A.alloc("dfo", [D], F32)
        wctr = [0]

        def load_w(bi):
            w = wst[wctr[0] % 3]
            ncw = FIN_BLOCKS[bi][2]
            O.dma("sp", f"ldwD{wctr[0] % 3}", [(w[:, :, 0:ncw], DR(FIN[l, bi][:, :, 0:ncw]))])
            wctr[0] += 1
            return w

        for gi in range(NG):
            t0 = gi * G
            for j in range(GT):
                xi = xin[j % 2]
                O.dma("sp", f"ldxD{j % 2}", [(xi, DR(XM[t0 + j * 128:t0 + (j + 1) * 128, :]))])
                psb = norm_T(xi, 128, n2w, xn[j % 2], sq, ss, None, PS[0])
                O.cp(rr(["act", "dve"]), xnT[:, :, j * 128:(j + 1) * 128], psb.v("p (k t) -> p k t", k=8))
            for b in range(6):
                wg = load_w(b)
                wu = load_w(6 + b)
                nch = 4 if b < 5 else 2
                for c in range(nch):
                    jj = b * 4 + c
                    pg, pu = PS[1 + (jj % 2) * 2], PS[2 + (jj % 2) * 2]
                    for kc in range(8):
                        O.mm(pg[:, 0:G], wg[:, kc, c * 128:(c + 1) * 128], xnT[:, kc, :], start=(kc == 0), stop=(kc == 7))
                    for kc in range(8):
                        O.mm(pu[:, 0:G], wu[:, kc, c * 128:(c + 1) * 128], xnT[:, kc, :], start=(kc == 0), stop=(kc == 7))
                    sg_ = sgt[jj % 2]
                    O.act(sg_, pg[:, 0:G], AF.Silu)
                    O.tt("dve", actT[:, jj, :], pu[:, 0:G], sg_, ALU.mult)
            for j in range(GT):
                tsl = slice(t0 + j * 128, t0 + (j + 1) * 128)
                O.dma("sp", "ldxr", [(xres, DR(XM[tsl, :]))])
                for cbk in range(2):
                    pd = PS[5 + cbk]
                    for jj in range(22):
                        O.mm(pd[:, 0:512], actT[:, jj, j * 128:(j + 1) * 128], fdn[:, cbk, jj, :], start=(jj == 0), stop=(jj == 21))
                    O.tt("dve", xo[:, cbk * 512:(cbk + 1) * 512], pd[:, 0:512], xres[:, cbk * 512:(cbk + 1) * 512], ALU.add)
                if not last:
                    O.dma("pool", "sxD", [(DR(X1[tsl, :]), xo)])
                else:
                    O.act(sq, xo, AF.Square, accum=ss)
                    head_rstd(ss, 1, 1.0 / D, EPS)
                    O.stt(fo, xo, ss[:, 0:1], fnwb, ALU.mult, ALU.mult)
                    O.dma("pool", "sfo", [(DR(out_d[tsl, :]), fo)])
        P.barrier()
        A.release(m)

    def halo_exchange(l):
        m = A.mark()
        O.dma("pool", "shx", [(DR(HX.rearrange("(r a) b -> r (a b)", r=2), "HX"), DR(X1[T - 2:T, :]))])
        P.barrier()
        P.cc("pool", lambda e: e.collective_compute("AllReduce", ALU.add, replica_groups=groups,
                                                     ins=[HX.opt()], outs=[HXr.opt()]), f"cch{l}", ["HX"], ["HXr"])
        P.barrier()
        h1 = A.alloc("h1", [D], F32); h2 = A.alloc("h2", [D], F32)
        O.dma("sp", "lhx", [(h1[0:2], DR(HXr.rearrange("(r a) b -> r (a b)", r=2), "HXr")),
                            (h2[0:2], DR(HX.rearrange("(r a) b -> r (a b)", r=2), "HX"))])
        O.tt("dve", h1[0:2], h1[0:2], h2[0:2], ALU.subtract)
        O.dma("pool", "shd", [(DR(HD[0:1, :], "HD"), h1[1:2]), (DR(HD[1:2, :], "HD"), h1[0:1])])
        P.barrier()
        A.release(m)

    prologue()
    for l in range(depth):
        layer_setup(l)
        xsrc = x_in if l == 0 else X1
        rh = None
        if paired:
            rh = xhalo if l == 0 else HD
        phaseA(l, xsrc, rh)
        if stop == "A":
            break
        scan_phase(l, "f")
        scan_phase(l, "b")
        if stop == "C":
            break
        phaseE(l, xsrc)
        if stop == "E":
            break
        phaseD(l, l == depth - 1)
        if paired and l < depth - 1:
            halo_exchange(l)
    P.barrier()
    P.emit()
    es.close()
    return nc


def pack_params(inp, depth):
    f = lambda a: np.asarray(a, np.float32)
    colp = np.zeros((depth, 128, NCOLP), np.float32)
    rowp = np.zeros((depth, 1, NROWP), np.float32)
    lbl = f(inp["hgrn_lb_logits"])
    for l in range(depth):
        cwv = f(inp["ssm_conv_w"])[l, :, 0, :]
        colp[l, :, 0:60] = cwv.reshape(5, 12, 128).transpose(2, 1, 0).reshape(128, 60)
        colp[l, :, 60:72] = f(inp["ssm_conv_b"])[l].reshape(12, 128).T
        colp[l, :, 72:100] = f(inp["rwkv_mu"])[l].reshape(2, 14, 128).transpose(2, 1, 0).reshape(128, 28)
        colp[l, :, 100:108] = f(inp["rwkv_w0"])[l].reshape(2, 4, 128).transpose(2, 0, 1).reshape(128, 8)
        colp[l, :, 108:112] = f(inp["rwkv_a0"])[l].reshape(4, 128).T
        colp[l, :, 112:116] = f(inp["rwkv_k_k"])[l].reshape(4, 128).T
        colp[l, :, 116:120] = f(inp["rwkv_k_a"])[l].reshape(4, 128).T
        colp[l, :, 120:124] = f(inp["rwkv_r_k"])[l].reshape(4, 128).T
        for l2 in range(min(depth, 2)):
            colp[l, :, 124 + 4 * l2:128 + 4 * l2] = lbl[l2].reshape(4, 128).T
        rowp[l, 0, 0:1024] = f(inp["norm1_w"])[l]
        rowp[l, 0, 1024:2048] = f(inp["norm2_w"])[l]
        rowp[l, 0, 2048:3072] = f(inp["ssm_norm_w"])[l]
        rowp[l, 0, 3072:3584] = f(inp["rwkv_ln_w"])[l]
        rowp[l, 0, 3584:4096] = f(inp["rwkv_ln_b"])[l]
        rowp[l, 0, 4096:4608] = f(inp["hgrn_norm_w"])[l]
        rowp[l, 0, 4608:4640] = f(inp["ssm_dt_bias"])[l].reshape(32)
        rowp[l, 0, 4640:4672] = f(inp["ssm_a_log"])[l].reshape(32)
        rowp[l, 0, 4672:4688] = f(inp["ssm_d"])[l]
    return colp, rowp


def make_in_map(inp, xs, depth):
    f = lambda a: np.ascontiguousarray(np.asarray(a, np.float32))
    colp, rowp = pack_params(inp, depth)
    m = {"x": f(xs), "colp": colp, "rowp": rowp, "fnw": f(inp["final_norm_w"]).reshape(1, D), "consts": CONSTS}
    for k in ("w_in", "w_branch_ssm", "w_branch_rwkv", "w_branch_hgrn", "w_out", "ffn_w_in", "ffn_w_down",
              "rwkv_w_up", "rwkv_a_up", "rwkv_g_up"):
        m[k] = f(inp[k])[:depth]
    return m


def flip_params(inp):
    o = dict(inp)
    w = np.array(inp["w_in"], np.float32, copy=True)
    w[:, :, 2560:2576], w[:, :, 2576:2592] = np.array(inp["w_in"])[:, :, 2576:2592], np.array(inp["w_in"])[:, :, 2560:2576]
    w[:, :, 4896:5408], w[:, :, 5408:5920] = np.array(inp["w_in"])[:, :, 5408:5920], np.array(inp["w_in"])[:, :, 4896:5408]
    o["w_in"] = w
    o["ssm_conv_w"] = np.asarray(inp["ssm_conv_w"])[:, ::-1]
    o["ssm_dt_bias"] = np.asarray(inp["ssm_dt_bias"])[:, ::-1]
    o["ssm_a_log"] = np.asarray(inp["ssm_a_log"])[:, ::-1]
    o["rwkv_mu"] = np.asarray(inp["rwkv_mu"])[:, ::-1]
    o["rwkv_w0"] = np.asarray(inp["rwkv_w0"])[:, ::-1]
    o["rwkv_w_up"] = np.asarray(inp["rwkv_w_up"])[:, ::-1]
    return o


def make_maps_paired(inputs, depth):
    x = np.asarray(inputs["x"], np.float32)
    B, L, _ = x.shape
    H = L // 2
    finp = flip_params(inputs)
    maps = []
    for b in range(B):
        ma = make_in_map(inputs, x[b, :H], depth)
        ma["xhalo"] = np.ascontiguousarray(x[b, H:H + 2])
        mb = make_in_map(finp, x[b, H:][::-1], depth)
        mb["xhalo"] = np.ascontiguousarray(x[b, H - 2:H][::-1])
        maps += [ma, mb]
    return maps


def kernel(**inputs):
    x = np.ascontiguousarray(np.asarray(inputs["x"], np.float32))
    B, L, _ = x.shape
    depth = int(np.asarray(inputs["w_in"]).shape[0])
    nc = build(L // 2, depth, npairs=B)
    maps = make_maps_paired(inputs, depth)
    res = run_bass_kernel_spmd(nc, maps, core_ids=list(range(2 * B)))
    outs = []
    for b in range(B):
        oa = np.asarray(res.results[2 * b]["out"], np.float32)
        ob = np.asarray(res.results[2 * b + 1]["out"], np.float32)[::-1]
        outs.append(np.concatenate([oa, ob], 0))
    return np.stack(outs).astype(np.float32)
```
